# Optimizing a Trainium2 kernel written in Bass

```python
import math
import jax, jax.numpy as jnp
from jax import lax
import numpy as np

D_MODEL = 1024
BATCH = 8
SEQ = 2048
DEPTH = 1
DEC_BATCH = 128
DEC_SEQ = 8
PAST_LEN = 8192
PAGE_SIZE = 128

WINDOW = 128
A_HEADS = 8
A_KV_HEADS = 2
A_HEAD_DIM = 64
A_GROUP = A_HEADS // A_KV_HEADS
B_HEADS = 4
B_KEY_DIM = 128
B_VAL_DIM = 128
CONV_W = 4
DN_CHUNK = 64
CONV_DIM = B_HEADS * (2 * B_KEY_DIM + B_VAL_DIM)
D_FF = 2816
EPS = 1e-6
IN_SIZES = (A_HEADS * A_HEAD_DIM, A_KV_HEADS * A_HEAD_DIM, A_KV_HEADS * A_HEAD_DIM,
            CONV_DIM, B_HEADS * B_VAL_DIM, B_HEADS, B_HEADS, D_MODEL, D_MODEL)
IN_DIM = sum(IN_SIZES)

kernel_name = 'hybrid_swa_sink_gated_deltanet_macaron_step'


def rms_norm(x, w):
    xf = x.astype(jnp.float32)
    y = xf * lax.rsqrt(jnp.mean(xf * xf, axis=-1, keepdims=True) + EPS)
    return (y * w.astype(jnp.float32)).astype(x.dtype)


def l2_normalize(x):
    return x * lax.rsqrt(jnp.sum(x * x, axis=-1, keepdims=True) + EPS)


def swiglu_ffn(x, w_up, w_down):
    gate, up = jnp.split(x @ w_up, 2, axis=-1)
    return (jax.nn.silu(gate) * up) @ w_down


def split_columns(z):
    offsets = [int(o) for o in np.cumsum(IN_SIZES)[:-1]]
    return jnp.split(z, offsets, axis=-1)


def sink_attend(q, k, v, mask, sinks):
    s = jnp.einsum('...qkgd,...skd->...kgqs', q, k, preferred_element_type=jnp.float32) * (A_HEAD_DIM ** -0.5)
    s = jnp.where(mask, s, -jnp.inf)
    sink = sinks.astype(jnp.float32).reshape(A_KV_HEADS, A_GROUP, 1, 1)
    m = jnp.maximum(jnp.max(s, axis=-1, keepdims=True), sink)
    e = jnp.exp(s - m)
    p = e / (jnp.sum(e, axis=-1, keepdims=True) + jnp.exp(sink - m))
    return jnp.einsum('...kgqs,...skd->...qkgd', p.astype(v.dtype), v)


def swa_prompt(q, k, v, sinks):
    bsz, seq, _ = q.shape
    nb = seq // WINDOW
    q = q.reshape(bsz, nb, WINDOW, A_KV_HEADS, A_GROUP, A_HEAD_DIM)
    pad = jnp.zeros((bsz, WINDOW, A_KV_HEADS, A_HEAD_DIM), k.dtype)
    kp = jnp.concatenate([pad, k], axis=1).reshape(bsz, nb + 1, WINDOW, A_KV_HEADS, A_HEAD_DIM)
    vp = jnp.concatenate([pad, v], axis=1).reshape(bsz, nb + 1, WINDOW, A_KV_HEADS, A_HEAD_DIM)
    kband = jnp.concatenate([kp[:, :-1], kp[:, 1:]], axis=2)
    vband = jnp.concatenate([vp[:, :-1], vp[:, 1:]], axis=2)
    n = jnp.arange(nb)[:, None, None]
    i = jnp.arange(WINDOW)[None, :, None]
    j = jnp.arange(2 * WINDOW)[None, None, :]
    diff = WINDOW + i - j
    mask = (diff >= 0) & (diff < WINDOW) & (n * WINDOW - WINDOW + j >= 0)
    o = sink_attend(q, kband, vband, mask[:, None, None], sinks)
    return o.reshape(bsz, seq, A_HEADS * A_HEAD_DIM), k[:, -WINDOW:], v[:, -WINDOW:]


def swa_sample(q, k, v, buf_k, buf_v, sinks):
    bsz, t, _ = q.shape
    wb = buf_k.shape[1]
    q = q.reshape(bsz, t, A_KV_HEADS, A_GROUP, A_HEAD_DIM)
    kc = jnp.concatenate([buf_k, k], axis=1)
    vc = jnp.concatenate([buf_v, v], axis=1)
    diff = (wb + jnp.arange(t))[:, None] - jnp.arange(wb + t)[None, :]
    mask = (diff >= 0) & (diff < WINDOW)
    o = sink_attend(q, kc, vc, mask, sinks)
    return o.reshape(bsz, t, A_HEADS * A_HEAD_DIM), kc[:, -wb:], vc[:, -wb:]


def causal_short_conv(x, buf, w):
    t = x.shape[1]
    xc = jnp.concatenate([buf, x], axis=1)
    out = sum(xc[:, j:j + t] * w[j] for j in range(CONV_W))
    return jax.nn.silu(out), xc[:, t:]


def gated_delta_chunked(q, k, v, g, beta, s0):
    bsz, t, h, dk = q.shape
    dv = v.shape[-1]
    c = DN_CHUNK if t % DN_CHUNK == 0 else t
    nc = t // c

    def blocks(x):
        return jnp.moveaxis(x.reshape((bsz, nc, c, h) + x.shape[3:]), 3, 2)

    q, k, v, g, beta = blocks(q), blocks(k), blocks(v), blocks(g), blocks(beta)
    gc = jnp.cumsum(g, axis=-1)
    incl = jnp.tril(jnp.ones((c, c), bool))
    strict = jnp.tril(jnp.ones((c, c), bool), -1)
    decay = jnp.exp(jnp.where(incl, gc[..., :, None] - gc[..., None, :], -jnp.inf))
    k_beta = k * beta[..., None]
    m = jnp.where(strict, jnp.einsum('...id,...jd->...ij', k_beta, k) * decay, 0.0)
    a = m + jnp.eye(c, dtype=m.dtype)
    rhs = jnp.concatenate([k_beta * jnp.exp(gc)[..., None], v * beta[..., None]], axis=-1)
    sol = lax.linalg.triangular_solve(a, rhs, left_side=True, lower=True, unit_diagonal=True)
    k_cum, v_u = sol[..., :dk], sol[..., dk:]
    qk = jnp.einsum('...id,...jd->...ij', q, k) * decay
    q_dec = q * jnp.exp(gc)[..., None]
    k_dec = k * jnp.exp(gc[..., -1:] - gc)[..., None]
    g_tot = jnp.exp(gc[..., -1])

    def step(s, xs):
        k_cum_n, v_u_n, qk_n, q_dec_n, k_dec_n, g_tot_n = xs
        v_new = v_u_n - jnp.einsum('bhcd,bhde->bhce', k_cum_n, s)
        o = jnp.einsum('bhcd,bhde->bhce', q_dec_n, s) + jnp.einsum('bhij,bhje->bhie', qk_n, v_new)
        s = s * g_tot_n[..., None, None] + jnp.einsum('bhcd,bhce->bhde', k_dec_n, v_new)
        return s, o

    xs = (jnp.moveaxis(k_cum, 1, 0), jnp.moveaxis(v_u, 1, 0), jnp.moveaxis(qk, 1, 0),
          jnp.moveaxis(q_dec, 1, 0), jnp.moveaxis(k_dec, 1, 0), jnp.moveaxis(g_tot, 1, 0))
    s_final, o = lax.scan(step, s0, xs)
    o = jnp.transpose(o, (1, 0, 3, 2, 4)).reshape(bsz, t, h, dv)
    return o, s_final


def parallel_mixers(h, w_in, attn_sinks, conv_w, dn_a_log, dn_dt_bias, dn_out_norm,
                    w_branch_a, w_branch_b, w_out, swa_k_buf, swa_v_buf, conv_buf, delta_state):
    bsz, t, _ = h.shape
    qa, ka, va, qkvb, zb, ab, bb, ga, gb = split_columns(h @ w_in)
    ka = ka.reshape(bsz, t, A_KV_HEADS, A_HEAD_DIM)
    va = va.reshape(bsz, t, A_KV_HEADS, A_HEAD_DIM)
    if swa_k_buf is None:
        oa, new_k, new_v = swa_prompt(qa, ka, va, attn_sinks)
    else:
        oa, new_k, new_v = swa_sample(qa, ka, va, swa_k_buf, swa_v_buf, attn_sinks)
    if conv_buf is None:
        conv_buf = jnp.zeros((bsz, CONV_W - 1, CONV_DIM), h.dtype)
    xb, new_conv = causal_short_conv(qkvb, conv_buf, conv_w)
    qb, kb, vb = jnp.split(xb.astype(jnp.float32), [B_HEADS * B_KEY_DIM, 2 * B_HEADS * B_KEY_DIM], axis=-1)
    qb = l2_normalize(qb.reshape(bsz, t, B_HEADS, B_KEY_DIM)) * (B_KEY_DIM ** -0.5)
    kb = l2_normalize(kb.reshape(bsz, t, B_HEADS, B_KEY_DIM))
    vb = vb.reshape(bsz, t, B_HEADS, B_VAL_DIM)
    g = -jnp.exp(dn_a_log.astype(jnp.float32)) * jax.nn.softplus(ab.astype(jnp.float32) + dn_dt_bias.astype(jnp.float32))
    beta = jax.nn.sigmoid(bb.astype(jnp.float32))
    if delta_state is None:
        s0 = jnp.zeros((bsz, B_HEADS, B_KEY_DIM, B_VAL_DIM), jnp.float32)
    else:
        s0 = delta_state.astype(jnp.float32)
    ob, new_s = gated_delta_chunked(qb, kb, vb, g, beta, s0)
    ob = rms_norm(ob, dn_out_norm) * jax.nn.silu(zb.astype(jnp.float32).reshape(bsz, t, B_HEADS, B_VAL_DIM))
    ob = ob.reshape(bsz, t, B_HEADS * B_VAL_DIM).astype(h.dtype)
    merged = jax.nn.sigmoid(ga) * (oa @ w_branch_a) + jax.nn.sigmoid(gb) * (ob @ w_branch_b)
    return merged @ w_out, new_k, new_v, new_conv, new_s.astype(h.dtype)


def decoder_layer(x, swa_k_buf, swa_v_buf, conv_buf, delta_state,
                  ffn1_norm_pre, ffn1_norm_post, ffn1_w_up, ffn1_w_down,
                  mix_norm_pre, mix_norm_post, w_in, attn_sinks, conv_w, dn_a_log, dn_dt_bias,
                  dn_out_norm, w_branch_a, w_branch_b, w_out,
                  ffn2_norm_pre, ffn2_norm_post, ffn2_w_up, ffn2_w_down):
    x = x + 0.5 * rms_norm(swiglu_ffn(rms_norm(x, ffn1_norm_pre), ffn1_w_up, ffn1_w_down), ffn1_norm_post)
    m, nk, nv, nc, ns = parallel_mixers(rms_norm(x, mix_norm_pre), w_in, attn_sinks, conv_w, dn_a_log,
                                        dn_dt_bias, dn_out_norm, w_branch_a, w_branch_b, w_out,
                                        swa_k_buf, swa_v_buf, conv_buf, delta_state)
    x = x + rms_norm(m, mix_norm_post)
    x = x + 0.5 * rms_norm(swiglu_ffn(rms_norm(x, ffn2_norm_pre), ffn2_w_up, ffn2_w_down), ffn2_norm_post)
    return x, nk, nv, nc, ns


def setup_inputs(seed: int = 0) -> dict:
    key = jax.random.key(seed)
    ks = iter(jax.random.split(key, 32))

    def nrm(shape, scale=1.0):
        return scale * jax.random.normal(next(ks), shape, jnp.float32)

    def gain(n):
        return 1.0 + 0.02 * nrm((DEPTH, n))

    win_buf = min(WINDOW, PAST_LEN)
    dt = jnp.exp(jax.random.uniform(next(ks), (DEPTH, B_HEADS), jnp.float32, math.log(1e-3), math.log(1e-1)))
    return {
        'x_prompt': nrm((BATCH, SEQ, D_MODEL)),
        'x_sample': nrm((DEC_BATCH, DEC_SEQ, D_MODEL)),
        'cache_swa_k': nrm((DEPTH, DEC_BATCH, win_buf, A_KV_HEADS, A_HEAD_DIM)),
        'cache_swa_v': nrm((DEPTH, DEC_BATCH, win_buf, A_KV_HEADS, A_HEAD_DIM)),
        'state_conv': nrm((DEPTH, DEC_BATCH, CONV_W - 1, CONV_DIM)),
        'state_delta': nrm((DEPTH, DEC_BATCH, B_HEADS, B_KEY_DIM, B_VAL_DIM), 0.1),
        'ffn1_norm_pre': gain(D_MODEL),
        'ffn1_norm_post': gain(D_MODEL),
        'ffn1_w_up': nrm((DEPTH, D_MODEL, 2 * D_FF), D_MODEL ** -0.5),
        'ffn1_w_down': nrm((DEPTH, D_FF, D_MODEL), D_FF ** -0.5),
        'mix_norm_pre': gain(D_MODEL),
        'mix_norm_post': gain(D_MODEL),
        'w_in': nrm((DEPTH, D_MODEL, IN_DIM), D_MODEL ** -0.5),
        'attn_sinks': nrm((DEPTH, A_HEADS)),
        'conv_w': nrm((DEPTH, CONV_W, CONV_DIM), CONV_W ** -0.5),
        'dn_a_log': jnp.log(jax.random.uniform(next(ks), (DEPTH, B_HEADS), jnp.float32, 1.0, 16.0)),
        'dn_dt_bias': jnp.log(jnp.expm1(dt)),
        'dn_out_norm': gain(B_VAL_DIM),
        'w_branch_a': nrm((DEPTH, A_HEADS * A_HEAD_DIM, D_MODEL), (A_HEADS * A_HEAD_DIM) ** -0.5),
        'w_branch_b': nrm((DEPTH, B_HEADS * B_VAL_DIM, D_MODEL), (B_HEADS * B_VAL_DIM) ** -0.5),
        'w_out': nrm((DEPTH, D_MODEL, D_MODEL), D_MODEL ** -0.5),
        'ffn2_norm_pre': gain(D_MODEL),
        'ffn2_norm_post': gain(D_MODEL),
        'ffn2_w_up': nrm((DEPTH, D_MODEL, 2 * D_FF), D_MODEL ** -0.5),
        'ffn2_w_down': nrm((DEPTH, D_FF, D_MODEL), D_FF ** -0.5),
    }


def reference(x_prompt, x_sample, cache_swa_k, cache_swa_v, state_conv, state_delta,
              ffn1_norm_pre, ffn1_norm_post, ffn1_w_up, ffn1_w_down,
              mix_norm_pre, mix_norm_post, w_in, attn_sinks, conv_w, dn_a_log, dn_dt_bias,
              dn_out_norm, w_branch_a, w_branch_b, w_out,
              ffn2_norm_pre, ffn2_norm_post, ffn2_w_up, ffn2_w_down):
    yp, ys = x_prompt, x_sample
    pk, pv, pc, ps = [], [], [], []
    sk, sv, sc, ss = [], [], [], []
    for l in range(DEPTH):
        lw = (ffn1_norm_pre[l], ffn1_norm_post[l], ffn1_w_up[l], ffn1_w_down[l],
              mix_norm_pre[l], mix_norm_post[l], w_in[l], attn_sinks[l], conv_w[l], dn_a_log[l],
              dn_dt_bias[l], dn_out_norm[l], w_branch_a[l], w_branch_b[l], w_out[l],
              ffn2_norm_pre[l], ffn2_norm_post[l], ffn2_w_up[l], ffn2_w_down[l])
        yp, k1, v1, c1, s1 = decoder_layer(yp, None, None, None, None, *lw)
        ys, k2, v2, c2, s2 = decoder_layer(ys, cache_swa_k[l], cache_swa_v[l], state_conv[l], state_delta[l], *lw)
        pk.append(k1); pv.append(v1); pc.append(c1); ps.append(s1)
        sk.append(k2); sv.append(v2); sc.append(c2); ss.append(s2)
    return (yp, ys,
            jnp.stack(pk), jnp.stack(pv), jnp.stack(pc), jnp.stack(ps),
            jnp.stack(sk), jnp.stack(sv), jnp.stack(sc), jnp.stack(ss))
```

```python
import numpy as np
import os
DBG = os.environ.get('DBG', '')
import concourse.bass as bass
import concourse.mybir as mybir
from concourse.bass_utils import run_bass_kernel_spmd

F32 = mybir.dt.float32
BF16 = mybir.dt.bfloat16
AF = mybir.ActivationFunctionType
ALU = mybir.AluOpType

D = 1024
DFF = 2816
NCORE = 8
EPS = 1e-6


class Buf:
    __slots__ = ('name', 'w', 'r', 'excl')

    def __init__(self, name='', excl=False):
        self.name = name
        self.w = None
        self.r = []
        self.excl = excl


class KB:
    ENG = ['pe', 'act', 'dve', 'pool', 'sp']
    BASE = {'pe': 0.03, 'act': 0.22, 'dve': 0.12, 'pool': 0.25, 'sp': 0.05}
    PER = {'pe': 1.0 / 2400, 'act': 1.0 / 1100, 'dve': 1.0 / 900, 'pool': 1.0 / 500, 'sp': 0.0}

    def __init__(self, nc, n_dma_sems=32, schedule=True):
        self.nc = nc
        self.schedule = schedule
        self.prog = {e: [] for e in self.ENG}
        self.sems = {}
        self.cnt = {}
        for e in self.ENG:
            self.sems['e_' + e] = nc.alloc_semaphore('s_' + e)
            self.cnt['e_' + e] = 0
        self.dma_sems = []
        for i in range(n_dma_sems):
            kk = 'd_%d' % i
            self.sems[kk] = nc.alloc_semaphore('s_' + kk)
            self.cnt[kk] = 0
            self.dma_sems.append(kk)
        self.dma_rr = {'sp': 0, 'pool': 0, 'act': 0}
        self.seen = {e: {} for e in self.ENG}
        self.nodes = []

    def op(self, eng, fn, reads=(), writes=(), n=512, c=None):
        ex = [b for b in reads if b.excl]
        if ex:
            reads = [b for b in reads if not b.excl]
            writes = list(writes) + ex
        if c is None:
            c = self.BASE[eng] + n * self.PER[eng]
        self.nodes.append(['op', eng, fn, list(reads), list(writes), c, c + 0.15])

    def dma(self, q, out, in_, reads=(), writes=(), nbytes=1 << 20):
        busy = 1.0 if q == 'pool' else 0.08
        if q == 'act':
            busy = 0.1
        lat = 2.0 + nbytes / 150e3
        self.nodes.append(['dma', q, (out, in_), list(reads), list(writes), busy, busy + lat])

    def barrier(self):
        self.nodes.append(['bar'])

    def _schedule_segment(self, seg):
        nn = len(seg)
        if nn <= 2 or not self.schedule:
            return list(range(nn))
        succs = [[] for _ in range(nn)]
        npred = [0] * nn
        lastw = {}
        readers = {}
        for i, nd in enumerate(seg):
            deps = set()
            for b in nd[3]:
                w = lastw.get(id(b))
                if w is not None:
                    deps.add(w)
            for b in nd[4]:
                w = lastw.get(id(b))
                if w is not None:
                    deps.add(w)
                for r in readers.get(id(b), ()):
                    deps.add(r)
            deps.discard(i)
            for d in deps:
                succs[d].append(i)
            npred[i] = len(deps)
            for b in nd[3]:
                readers.setdefault(id(b), []).append(i)
            for b in nd[4]:
                lastw[id(b)] = i
                readers[id(b)] = []
        prio = [0.0] * nn
        for i in range(nn - 1, -1, -1):
            m = 0.0
            for s_ in succs[i]:
                if prio[s_] > m:
                    m = prio[s_]
            prio[i] = seg[i][6] + m
        rt = [0.0] * nn
        free_at = {e: 0.0 for e in self.ENG}
        ready = {e: [] for e in self.ENG}
        import bisect
        for i in range(nn):
            if npred[i] == 0:
                ready[seg[i][1]].append(i)
        order = []
        W = 24
        while len(order) < nn:
            best = None
            for e in self.ENG:
                rl = ready[e]
                if not rl:
                    continue
                fa = free_at[e]
                for i in rl[:W]:
                    st = rt[i] if rt[i] > fa else fa
                    key = (st, -prio[i], i)
                    if best is None or key < best[0]:
                        best = (key, e, i)
            key, e, i = best
            st = key[0]
            ready[e].remove(i)
            nd = seg[i]
            free_at[e] = st + nd[5]
            fin = st + nd[6]
            order.append(i)
            for s_ in succs[i]:
                if fin > rt[s_]:
                    rt[s_] = fin
                npred[s_] -= 1
                if npred[s_] == 0:
                    bisect.insort(ready[seg[s_][1]], s_)
        self.est_time = getattr(self, 'est_time', 0.0) + max(free_at.values())
        return order

    def _wait(self, eng, tok):
        if tok is None:
            return
        sk, val = tok
        if eng == 'pe' and sk == 'e_pe':
            return
        if self.seen[eng].get(sk, 0) >= val:
            return
        self.seen[eng][sk] = val
        self.prog[eng].append(('wait', self.sems[sk], val))

    def _deps(self, eng, reads, writes):
        for b in reads:
            self._wait(eng, b.w)
        for b in writes:
            self._wait(eng, b.w)
            for t in b.r:
                self._wait(eng, t)

    def _commit(self, tok, reads, writes):
        for b in reads:
            b.r.append(tok)
            if len(b.r) > 48:
                last = {}
                for t in b.r:
                    if last.get(t[0], 0) < t[1]:
                        last[t[0]] = t[1]
                b.r = list(last.items())
        for b in writes:
            b.w = tok
            b.r = []

    def _emit_op(self, nd):
        _, eng, fn, reads, writes = nd[:5]
        self._deps(eng, reads, writes)
        sk = 'e_' + eng
        self.cnt[sk] += 1
        tok = (sk, self.cnt[sk])
        self.prog[eng].append(('op', fn, self.sems[sk]))
        self._commit(tok, reads, writes)

    def _emit_dma(self, nd):
        _, q, (out, in_), reads, writes = nd[:5]
        self._deps(q, reads, writes)
        half = len(self.dma_sems) // 2
        if q == 'pool':
            base, span = half, half
        elif q == 'sp':
            base, span = 0, half - 4
        else:
            base, span = half - 4, 4
        sk = self.dma_sems[base + self.dma_rr[q]]
        self.dma_rr[q] = (self.dma_rr[q] + 1) % span
        if self.cnt[sk] > 0:
            self._wait(q, (sk, self.cnt[sk]))
        self.cnt[sk] += 16
        tok = (sk, self.cnt[sk])
        if q == 'pool':
            hist = self.__dict__.setdefault('pool_hist', [])
            if len(hist) >= 4:
                self._wait(q, hist[-4])
            hist.append(tok)
        self.prog[q].append(('dma', out, in_, self.sems[sk]))
        self._commit(tok, reads, writes)

    def _emit_barrier(self):
        for e in self.ENG:
            for sk, c in self.cnt.items():
                if c > 0:
                    self._wait(e, (sk, c))

    def emit(self):
        seg = []
        for nd in self.nodes + [['bar']]:
            if nd[0] == 'bar':
                for i in self._schedule_segment(seg):
                    if seg[i][0] == 'op':
                        self._emit_op(seg[i])
                    else:
                        self._emit_dma(seg[i])
                self._emit_barrier()
                seg = []
            else:
                seg.append(nd)
        nc = self.nc
        prog = self.prog

        def run(engname, e):
            for it in prog[engname]:
                if it[0] == 'wait':
                    e.wait_ge(it[1], it[2])
                elif it[0] == 'op':
                    it[1](e).then_inc(it[2], 1)
                else:
                    e.dma_start(out=it[1], in_=it[2]).then_inc(it[3], 16)

        with nc.Block() as block:
            @block.tensor
            def _(e):
                run('pe', e)

            @block.scalar
            def _(e):
                run('act', e)

            @block.vector
            def _(e):
                run('dve', e)

            @block.gpsimd
            def _(e):
                run('pool', e)

            @block.sync
            def _(e):
                run('sp', e)


class Arena:
    def __init__(self, t, nwords):
        self.t = t
        self.n = nwords
        self.top = 0

    def alloc(self, shape, dtype):
        nel = int(np.prod(shape))
        nbytes = nel * (2 if dtype == BF16 else 4)
        nw = (nbytes + 3) // 4
        nw = (nw + 7) // 8 * 8
        off = self.top
        self.top += nw
        assert self.top <= self.n, ('arena overflow', self.top, self.n)
        ap = self.t[:, off:off + nw]
        if dtype == BF16:
            ap = ap.bitcast(BF16)
        ap = ap[:, 0:nel]
        if len(shape) > 1:
            names = ['d%d' % i for i in range(len(shape))]
            pat = 'p (' + ' '.join(names) + ') -> p ' + ' '.join(names)
            ap = ap.rearrange(pat, **{n: s for n, s in zip(names, shape)})
        return ap

    def mark(self):
        return self.top

    def release(self, m):
        self.top = m


def bc(ap, shape):
    return ap.broadcast_to(shape)


def make_consts():
    i = np.arange(128)
    c = {}
    c['ident'] = np.eye(128, dtype=np.float32)
    c['ones'] = np.ones((128, 128), np.float32)
    c['U_p'] = (i[:, None] <= i[None, :]).astype(np.float32)
    c['SL_p'] = (i[:, None] > i[None, :]).astype(np.float32)
    c['MB_p'] = np.where(i[None, :] > i[:, None], 0.0, -30000.0).astype(np.float32)
    blk = i // 8
    same = blk[:, None] == blk[None, :]
    c['U_s'] = (same & (i[:, None] <= i[None, :])).astype(np.float32)
    c['SL_s'] = (same & (i[:, None] > i[None, :])).astype(np.float32)
    c['MB_s'] = np.where(same & (i[None, :] > i[:, None]), 0.0, -30000.0).astype(np.float32)
    c['M_cur'] = (i[:, None] <= i[None, :]).astype(np.float32)
    c['M_prev'] = (i[:, None] > i[None, :]).astype(np.float32)
    c['M_blk'] = (same & (i[:, None] <= i[None, :])).astype(np.float32)
    mc = np.zeros((128, 128), np.float32)
    mc[:, 0:8] = (i[:, None] >= (np.arange(8)[None, :] + 1)).astype(np.float32)
    c['M_cache'] = mc
    bm = np.zeros((128, 128), np.float32)
    bm[:, 0:16] = (blk[:, None] == np.arange(16)[None, :]).astype(np.float32)
    c['BM'] = bm
    names = ['ident', 'ones', 'U_p', 'SL_p', 'MB_p', 'U_s', 'SL_s', 'MB_s', 'M_cur', 'M_prev', 'M_blk', 'M_cache', 'BM']
    arr = np.stack([c[n] for n in names], axis=1)
    cm = np.broadcast_to((np.arange(16)[:, None] == blk[None, :]).astype(np.float32)[None], (128, 16, 128))
    return names, np.ascontiguousarray(arr), np.ascontiguousarray(cm)


CONST_NAMES = ['ident', 'ones', 'U_p', 'SL_p', 'MB_p', 'U_s', 'SL_s', 'MB_s', 'M_cur', 'M_prev', 'M_blk', 'M_cache', 'BM']


def build_nc(stop_after=None, skip_ffn=False):
    nc = bass.Bass("TRN2", target_bir_lowering=False)

    def din(name, shape):
        return nc.dram_tensor(name, list(shape), F32, kind="ExternalInput").ap()

    def dout(name, shape):
        return nc.dram_tensor(name, list(shape), F32, kind="ExternalOutput").ap()

    x_p = din('x_p', [2048, D])
    x_s = din('x_s', [128, D])
    cache_k = din('cache_k', [16, 128, 128])
    cache_v = din('cache_v', [16, 128, 128])
    st_conv = din('st_conv', [48, 1536])
    st_delta = din('st_delta', [16, 4, 128, 128])
    consts_d = din('consts', [128, len(CONST_NAMES), 128])
    colmask_d = din('colmask', [128, 16, 128])
    gpost_d = din('gpost', [128, 3, D])
    gpreT_d = din('gpreT', [128, 3, 8])
    smallp_d = din('smallp', [128, 32])
    convw_d = din('convw', [128, 12, 4])
    w_up = [din('w_up1', [D, 2 * DFF]), din('w_up2', [D, 2 * DFF])]
    w_down = [din('w_down1', [DFF, D]), din('w_down2', [DFF, D])]
    w_fm = din('w_fm', [D, 4736])
    w_tm = din('w_tm', [D, 320])
    w_ba = din('w_ba', [512, D])
    w_bb = din('w_bb', [512, D])
    w_o = din('w_o', [D, D])

    y_p = dout('y_p', [2048, D])
    y_s = dout('y_s', [128, D])
    nk_p = dout('nk_p', [128, 128])
    nv_p = dout('nv_p', [128, 128])
    ncv_p = dout('ncv_p', [3, 1536])
    nd_p = dout('nd_p', [4, 128, 128])
    nk_s = dout('nk_s', [16, 128, 128])
    nv_s = dout('nv_s', [16, 128, 128])
    ncv_s = dout('ncv_s', [16, 3, 1536])
    nd_s = dout('nd_s', [16, 4, 128, 128])

    k = KB(nc)
    NW = 53000
    arena_t = nc.alloc_sbuf_tensor("arena", [128, NW], F32)
    A = Arena(arena_t, NW)
    ps = nc.alloc_psum_tensor("ps", [128, 8, 512], F32)
    PB = [Buf('ps%d' % i, excl=True) for i in range(8)]

    def psb(b, n=512):
        return ps[:, b, 0:n]

    def psb16(b):
        return ps[:, b, :].bitcast(BF16)

    NC_ = len(CONST_NAMES)
    cf = A.alloc([NC_, 128], F32)
    cfB = Buf('cf')
    CI = {n: i for i, n in enumerate(CONST_NAMES)}

    def C(n):
        return cf[:, CI[n], :]
    cb = A.alloc([NC_, 128], BF16)
    cbB = Buf('cb')

    def Cb(n):
        return cb[:, CI[n], :]
    gpost = A.alloc([3, D], F32)
    gpostB = Buf()
    gpreT = A.alloc([3, 8], F32)
    gpreTB = Buf()
    smallp = A.alloc([32], F32)
    smallpB = Buf()
    negA = A.alloc([4], F32)
    esink = A.alloc([8], F32)
    derivB = Buf()
    convw = A.alloc([12, 4], F32)
    convwB = Buf()
    NTMAX = 6
    X = A.alloc([NTMAX, D], F32)
    XB = [Buf('X%d' % i) for i in range(NTMAX)]
    hT = A.alloc([8, NTMAX * 128], BF16)
    hTB = [Buf('hT%d' % i) for i in range(NTMAX)]
    S = A.alloc([4, 128], F32)
    Sbf = A.alloc([4, 128], BF16)
    SB = Buf('S')
    SbfB = Buf('Sbf')
    convhist = A.alloc([12, 3], F32)
    convhistB = Buf('convhist')
    kT = A.alloc([128 + NTMAX * 128], BF16)
    kTB = [Buf('kT%d' % i) for i in range(NTMAX + 1)]
    KV = A.alloc([NTMAX + 1, 256], BF16)
    KVB = [Buf('KV%d' % i) for i in range(NTMAX + 1)]
    WS = [A.alloc([8, 256], BF16) for _ in range(6)]
    WSB = [Buf('ws%d' % i) for i in range(6)]
    persist_mark = A.mark()

    k.dma('sp', cf, consts_d[:, :, :], writes=[cfB])
    k.dma('sp', gpost, gpost_d[:, :, :], writes=[gpostB])
    k.dma('sp', gpreT, gpreT_d[:, :, :], writes=[gpreTB])
    k.dma('sp', smallp, smallp_d[:, :], writes=[smallpB])
    k.dma('sp', convw, convw_d[:, :, :], writes=[convwB])
    k.op('dve', lambda e: e.tensor_copy(out=cb, in_=cf), reads=[cfB], writes=[cbB])
    k.op('act', lambda e: e.activation(out=negA, in_=smallp[:, 0:4], func=AF.Exp), reads=[smallpB], writes=[derivB])
    k.op('act', lambda e: e.activation(out=esink, in_=smallp[:, 8:16], func=AF.Exp), reads=[smallpB], writes=[derivB])
    k.op('dve', lambda e: e.tensor_scalar(out=negA, in0=negA, scalar1=-1.0, scalar2=None, op0=ALU.mult),
         reads=[derivB], writes=[derivB])
    k.op('pool', lambda e: e.memset(S, 0.0), writes=[SB])
    k.op('pool', lambda e: e.memset(Sbf, 0.0), writes=[SbfB])
    k.op('pool', lambda e: e.memset(convhist, 0.0), writes=[convhistB])

    def tok_groups(T):
        g = []
        o = 0
        r = T % 512
        if r:
            g.append((0, r))
            o = r
        while o < T:
            g.append((o, 512))
            o += 512
        return g

    def prenorm(nt, which, ctx=None):
        if ctx is not None and 'pn' in ctx:
            sq, sqB, ssq, rstd, stB, hb, hbB = ctx['pn']
        else:
            sq = [A.alloc([D], F32) for _ in range(2)]
            sqB = [Buf(), Buf()]
            ssq = A.alloc([NTMAX], F32)
            rstd = A.alloc([NTMAX], F32)
            stB = [Buf() for _ in range(NTMAX)]
            hb = A.alloc([2, D], BF16)
            hbB = [Buf(), Buf()]
            if ctx is not None:
                ctx['pn'] = (sq, sqB, ssq, rstd, stB, hb, hbB)
        for t in range(nt):
            s = t % 2
            k.op('act', lambda e, t=t, s=s: e.activation(out=sq[s], in_=X[:, t, :], func=AF.Square, accum_out=ssq[:, t:t + 1]),
                 reads=[XB[t]], writes=[sqB[s], stB[t]], n=1024)
            k.op('act', lambda e, t=t: e.activation(out=rstd[:, t:t + 1], in_=ssq[:, t:t + 1], func=AF.Sqrt, bias=EPS,
                                                    scale=1.0 / D), reads=[stB[t]], writes=[stB[t]], n=1)
            k.op('dve', lambda e, t=t: e.reciprocal(out=rstd[:, t:t + 1], in_=rstd[:, t:t + 1]), reads=[stB[t]], writes=[stB[t]], n=1)
            k.op('dve', lambda e, t=t, s=s: e.tensor_scalar(out=hb[:, s, :], in0=X[:, t, :], scalar1=rstd[:, t:t + 1],
                                                            scalar2=None, op0=ALU.mult),
                 reads=[XB[t], stB[t]], writes=[hbB[s]], n=1024)
            bank = 6 + s
            for c in range(8):
                k.op('pe', lambda e, c=c, s=s, bank=bank: e.transpose(
                    out=psb16(bank)[:, c * 128:(c + 1) * 128], in_=hb[:, s, c * 128:(c + 1) * 128], identity=Cb('ident')),
                    reads=[hbB[s], cbB], writes=[PB[bank]], n=128)
            k.op('dve', lambda e, t=t, bank=bank: e.tensor_tensor(
                out=hT[:, :, t * 128:(t + 1) * 128], in0=psb16(bank).rearrange("p (c n) -> p c n", n=128),
                in1=bc(gpreT[:, which, :].unsqueeze(2), [128, 8, 128]), op=ALU.mult),
                reads=[PB[bank], gpreTB], writes=[hTB[t]], n=1024)

    def make_post(ctx=None):
        if ctx is not None and 'post' in ctx:
            junk, ss2, r, tmp, jB, sB, tB = ctx['post']
        else:
            junk = A.alloc([512], F32)
            ss2 = A.alloc([2], F32)
            r = A.alloc([1], F32)
            tmp = A.alloc([D], F32)
            jB, sB, tB = Buf(), Buf(), Buf()
            if ctx is not None:
                ctx['post'] = (junk, ss2, r, tmp, jB, sB, tB)

        def post(t, banks, which, half):
            for hh in range(2):
                k.op('act', lambda e, hh=hh: e.activation(out=junk, in_=psb(banks[hh]), func=AF.Square,
                                                          accum_out=ss2[:, hh:hh + 1]),
                     reads=[PB[banks[hh]]], writes=[jB, sB], n=512)
            k.op('dve', lambda e: e.tensor_tensor(out=r, in0=ss2[:, 0:1], in1=ss2[:, 1:2], op=ALU.add),
                 reads=[sB], writes=[sB])
            sc = 4.0 if half else 1.0
            k.op('act', lambda e: e.activation(out=r, in_=r, func=AF.Sqrt, bias=EPS * sc, scale=sc / D),
                 reads=[sB], writes=[sB])
            k.op('dve', lambda e: e.reciprocal(out=r, in_=r), reads=[sB], writes=[sB])
            for hh in range(2):
                k.op('dve', lambda e, hh=hh: e.scalar_tensor_tensor(
                    out=tmp[:, hh * 512:(hh + 1) * 512], in0=psb(banks[hh]), scalar=r[:, 0:1],
                    in1=gpost[:, which, hh * 512:(hh + 1) * 512], op0=ALU.mult, op1=ALU.mult),
                    reads=[PB[banks[hh]], sB, gpostB], writes=[tB])
            k.op('pool', lambda e: e.tensor_tensor(out=X[:, t, :], in0=X[:, t, :], in1=tmp, op=ALU.add),
                 reads=[tB, XB[t]], writes=[XB[t]], n=1024)
        return post

    FCTX = {}

    def ffn_wload(which, fb):
        s = fb % 3
        wup = w_up[which]
        k.dma('pool', WS[2 * s], wup[:, fb * 256:(fb + 1) * 256].rearrange("(c p) n -> p c n", p=128), writes=[WSB[2 * s]])
        k.dma('pool', WS[2 * s + 1], wup[:, DFF + fb * 256:DFF + (fb + 1) * 256].rearrange("(c p) n -> p c n", p=128),
              writes=[WSB[2 * s + 1]])

    def ffn(nt, which, pre=0):
        T = nt * 128
        m = A.mark()
        assert m == persist_mark
        prenorm(nt, 0 if which == 0 else 2, FCTX)
        post = make_post(FCTX)
        WDG = [(0, 6), (6, 12), (12, 17), (17, 22)]
        wdOf = [0] * 6 + [1] * 6 + [2] * 5 + [3] * 5
        if 'main' in FCTX:
            actT, actB, wd, wdB, sg, sgB, top_ = FCTX['main']
            A.top = top_
        else:
            actT = A.alloc([22, NTMAX * 128], BF16)
            actB = [Buf('act%d' % j) for j in range(22)]
            wd = A.alloc([22, D], BF16)
            wdB = [Buf('wd%d' % i) for i in range(4)]
        NWB = 3
        wg = [WS[2 * i] for i in range(NWB)]
        wu = [WS[2 * i + 1] for i in range(NWB)]
        wgB = [WSB[2 * i] for i in range(NWB)]
        wuB = [WSB[2 * i + 1] for i in range(NWB)]
        if 'main' not in FCTX:
            sg = [A.alloc([512], BF16) for _ in range(2)]
            sgB = [Buf(), Buf()]
            FCTX['main'] = (actT, actB, wd, wdB, sg, sgB, A.top)
        groups = tok_groups(T)
        wup = w_up[which]
        for gi_, (j0, j1) in enumerate(WDG):
            k.dma('pool', wd[:, j0:j1, :], w_down[which][j0 * 128:j1 * 128, :].rearrange("(j p) d -> p j d", p=128),
                  writes=[wdB[gi_]], nbytes=(j1 - j0) * 128 * 4096)
        gi = 0
        for fb in range(11):
            s = fb % NWB
            if fb >= pre:
                ffn_wload(which, fb)
            for jj in range(2):
                j = 2 * fb + jj
                for (o, n) in groups:
                    bg = (gi % 3) * 2
                    bu = bg + 1
                    ss_ = gi % 2
                    gi += 1
                    for c in range(8):
                        k.op('pe', lambda e, c=c, s=s, jj=jj, o=o, n=n, bg=bg: e.matmul(
                            psb(bg, n), lhsT=wg[s][:, c, jj * 128:(jj + 1) * 128], rhs=hT[:, c, o:o + n],
                            start=(c == 0), stop=(c == 7)),
                            reads=[wgB[s]] + hTB[o // 128:(o + n) // 128], writes=[PB[bg]], n=n)
                    for c in range(8):
                        k.op('pe', lambda e, c=c, s=s, jj=jj, o=o, n=n, bu=bu: e.matmul(
                            psb(bu, n), lhsT=wu[s][:, c, jj * 128:(jj + 1) * 128], rhs=hT[:, c, o:o + n],
                            start=(c == 0), stop=(c == 7)),
                            reads=[wuB[s]] + hTB[o // 128:(o + n) // 128], writes=[PB[bu]], n=n)
                    k.op('act', lambda e, bg=bg, n=n, ss_=ss_: e.activation(out=sg[ss_][:, 0:n], in_=psb(bg, n), func=AF.Silu),
                         reads=[PB[bg]], writes=[sgB[ss_]], n=n)
                    k.op('dve', lambda e, bu=bu, n=n, o=o, j=j, ss_=ss_: e.tensor_tensor(
                        out=actT[:, j, o:o + n], in0=psb(bu, n), in1=sg[ss_][:, 0:n], op=ALU.mult),
                        reads=[PB[bu], sgB[ss_]], writes=[actB[j]], n=n)
        for t in range(nt):
            banks = [6, 7] if t % 2 == 0 else [4, 5]
            for hh in range(2):
                for j in range(22):
                    k.op('pe', lambda e, t=t, hh=hh, j=j, banks=banks: e.matmul(
                        psb(banks[hh]), lhsT=actT[:, j, t * 128:(t + 1) * 128], rhs=wd[:, j, hh * 512:(hh + 1) * 512],
                        start=(j == 0), stop=(j == 21)),
                        reads=[actB[j], wdB[wdOf[j]]], writes=[PB[banks[hh]]])
            post(t, banks, 0 if which == 0 else 2, True)
        A.release(m)


    FM_Q, FM_K, FM_QKV, FM_ZB, FM_GA, FM_GB = 0, 512, 640, 2176, 2688, 3712
    ISQ = 128 ** -0.5

    M1_BLK_ORDER = [2, 3, 4, 5, 6, 7, 8, 0, 1, 9, 10]

    def m1_wload(pos):
        blk = M1_BLK_ORDER[pos]
        s = pos % 3
        wcols = min(256, 2688 - blk * 256)
        k.dma('pool', WS[s][:, :, 0:wcols], w_fm[:, blk * 256:blk * 256 + wcols].rearrange("(c p) n -> p c n", p=128),
              writes=[WSB[s]])

    def m3_wload(blk):
        s = blk % 2
        k.dma('pool', WS[s], w_fm[:, FM_GA + blk * 256:FM_GA + (blk + 1) * 256].rearrange("(c p) n -> p c n", p=128),
              writes=[WSB[s]])
        k.dma('pool', WS[2 + s], w_fm[:, FM_GB + blk * 256:FM_GB + (blk + 1) * 256].rearrange("(c p) n -> p c n", p=128),
              writes=[WSB[2 + s]])
        k.dma('pool', WS[4][:, 4 * s:4 * s + 4, :], w_ba[:, blk * 256:(blk + 1) * 256].rearrange("(c p) n -> p c n", p=128),
              writes=[WSB[4]], nbytes=1 << 19)
        k.dma('pool', WS[5][:, 4 * s:4 * s + 4, :], w_bb[:, blk * 256:(blk + 1) * 256].rearrange("(c p) n -> p c n", p=128),
              writes=[WSB[5]], nbytes=1 << 19)

    def mixer(nt, npt, has_s, t0, pre1=0, pre3=0, next_ffn_pre=0):
        T = nt * 128
        Tp = npt * 128
        groups = tok_groups(T)
        m0 = A.mark()
        qT = A.alloc([4, NTMAX * 128], BF16)
        qTB = [Buf() for _ in range(NTMAX)]
        KQV = A.alloc([NTMAX, 4, 3, 128], BF16)
        KQVB = [Buf() for _ in range(NTMAX)]
        zsT = A.alloc([4, NTMAX * 128], BF16)
        zsB = [Buf() for _ in range(NTMAX)]
        oaT = A.alloc([4, NTMAX * 128], BF16)
        oaB = [Buf() for _ in range(NTMAX)]
        obT = A.alloc([4, NTMAX * 128], BF16)
        obB = [Buf() for _ in range(NTMAX)]
        gbraw = A.alloc([NTMAX, 8], F32)
        gcol = A.alloc([NTMAX, 4], F32)
        betac = A.alloc([NTMAX, 4], F32)
        gbB = Buf()
        gB = Buf()
        outB = Buf()
        outB2 = [Buf(), Buf()]

        def tb(bl, o, n):
            return bl[o // 128:(o + n + 127) // 128]

        m1 = A.mark()
        kvout = A.alloc([2, 256], F32)
        kvoB = [Buf(), Buf()]
        cvout = A.alloc([2, 1536], F32)
        cvoB = [Buf(), Buf()]
        prenorm(nt, 1)
        wtm = A.alloc([8, 320], BF16)
        wtmB = Buf()
        k.dma('pool', wtm, w_tm.rearrange("(c p) n -> p c n", p=128), writes=[wtmB])
        NWB = 3
        wblk = [WS[i] for i in range(NWB)]
        wblkB = [WSB[i] for i in range(NWB)]
        st = [A.alloc([3 + NTMAX * 128], F32) for _ in range(2)]
        stB = [Buf(), Buf()]
        acc = [A.alloc([NTMAX * 128], F32) for _ in range(2)]
        accB = [Buf(), Buf()]
        sil2 = [A.alloc([NTMAX * 128], F32) for _ in range(2)]
        silB2 = [Buf(), Buf()]
        sqb2 = [A.alloc([NTMAX * 128], BF16) for _ in range(2)]
        sqbB2 = [Buf(), Buf()]
        rn2 = [A.alloc([NTMAX * 128], F32) for _ in range(2)]
        rnB2 = [Buf(), Buf()]
        if has_s:
            ss = A.alloc([16, 11], F32)
            ssB = Buf()
            scraw = A.alloc([1536], F32)
            scrawB = Buf()
            scT = A.alloc([12, 48], F32)
            scTB = Buf()
            k.dma('sp', scraw[0:48, :], st_conv[:, :], writes=[scrawB])
            for half in (range(2) if 'T' not in DBG else []):
                for c6 in range(6):
                    ch = half * 6 + c6
                    k.op('pe', lambda e, ch=ch, c6=c6, half=half: e.transpose(
                        out=ps[:, half, c6 * 48:(c6 + 1) * 48], in_=scraw[0:48, ch * 128:(ch + 1) * 128],
                        identity=C('ident')[0:48, 0:48]), reads=[scrawB, cfB], writes=[PB[half]])
                k.op('dve', lambda e, half=half: e.tensor_copy(
                    out=scT[:, half * 6:(half + 1) * 6, :], in_=ps[:, half, 0:288].rearrange("p (c n) -> p c n", n=48)),
                    reads=[PB[half]], writes=[scTB])

        special = {}
        if t0 + npt == 16:
            special[npt - 1] = 0
        if has_s:
            special[npt] = 1

        for t in (range(nt) if 'M' not in DBG else []):
            bank = 6 + (t % 2)
            for c in range(8):
                k.op('pe', lambda e, t=t, c=c, bank=bank: e.matmul(
                    psb(bank, 256), lhsT=hT[:, c, t * 128:(t + 1) * 128], rhs=wtm[:, c, 0:256], start=(c == 0), stop=(c == 7)),
                    reads=[hTB[t], wtmB], writes=[PB[bank]])
            for c in (range(8) if 'm' not in DBG else []):
                k.op('pe', lambda e, t=t, c=c, bank=bank: e.matmul(
                    ps[:, bank, 256:320], lhsT=hT[:, c, t * 128:(t + 1) * 128], rhs=wtm[:, c, 256:320], start=(c == 0), stop=(c == 7)),
                    reads=[hTB[t], wtmB], writes=[PB[bank]])
            if 'e' in DBG:
                continue
            k.op('act', lambda e, t=t, bank=bank: e.copy(out=KV[:, 1 + t, :], in_=psb(bank, 256)),
                 reads=[PB[bank]], writes=[KVB[1 + t]])
            if 'd' in DBG:
                continue
            if 'm' not in DBG:
                k.op('act', lambda e, t=t, bank=bank: e.copy(out=gbraw[:, t, :], in_=ps[:, bank, (0 if 'z' in DBG else 256):(8 if 'z' in DBG else 264)]),
                     reads=[PB[bank]], writes=[gbB])
            if t in special:
                si = special[t]
                k.op('dve', lambda e, si=si, bank=bank: e.tensor_copy(out=kvout[:, si, :], in_=psb(bank, 256)),
                     reads=[PB[bank]], writes=[kvoB[si]])
        if 'G' not in DBG:
            k.op('dve', lambda e: e.tensor_tensor(out=gcol[:, 0:nt, :], in0=gbraw[:, 0:nt, 0:4],
                                                  in1=bc(smallp[:, 4:8].unsqueeze(1), [128, nt, 4]), op=ALU.add),
                 reads=[gbB, smallpB], writes=[gB])
            k.op('act', lambda e: e.activation(out=gcol[:, 0:nt, :], in_=gcol[:, 0:nt, :], func=AF.Exp), reads=[gB], writes=[gB])
            k.op('act', lambda e: e.activation(out=gcol[:, 0:nt, :], in_=gcol[:, 0:nt, :], func=AF.Ln, bias=1.0),
                 reads=[gB], writes=[gB])
            k.op('dve', lambda e: e.tensor_tensor(out=gcol[:, 0:nt, :], in0=gcol[:, 0:nt, :],
                                                  in1=bc(negA.unsqueeze(1), [128, nt, 4]), op=ALU.mult),
                 reads=[gB, derivB], writes=[gB])
            k.op('act', lambda e: e.activation(out=betac[:, 0:nt, :], in_=gbraw[:, 0:nt, 4:8], func=AF.Sigmoid),
                 reads=[gbB], writes=[gB])

        gi = 0
        ci_order = [ci_ for b_ in M1_BLK_ORDER for ci_ in (2 * b_, 2 * b_ + 1) if ci_ < 21]
        started = set()
        for ci in (ci_order if 'F' not in DBG else []):
            blk = ci // 2
            pos = M1_BLK_ORDER.index(blk)
            s = pos % NWB
            if blk not in started:
                started.add(blk)
                if pos >= pre1:
                    m1_wload(pos)
            wv = wblk[s][:, :, (ci % 2) * 128:(ci % 2) * 128 + 128]
            ch = ci - 5
            sidx = ch % 2
            for (o, n) in groups:
                bank = gi % 6
                gi += 1
                for c in range(8):
                    k.op('pe', lambda e, c=c, wv=wv, o=o, n=n, bank=bank: e.matmul(
                        psb(bank, n), lhsT=wv[:, c, :], rhs=hT[:, c, o:o + n], start=(c == 0), stop=(c == 7)),
                        reads=[wblkB[s]] + tb(hTB, o, n), writes=[PB[bank]], n=n)
                if ci < 4:
                    k.op('act', lambda e, ci=ci, o=o, n=n, bank=bank: e.activation(
                        out=qT[:, ci, o:o + n], in_=psb(bank, n), func=AF.Copy, scale=0.125),
                        reads=[PB[bank]], writes=tb(qTB, o, n))
                elif ci == 4:
                    k.op('act', lambda e, o=o, n=n, bank=bank: e.copy(out=kT[:, 128 + o:128 + o + n], in_=psb(bank, n)),
                         reads=[PB[bank]], writes=tb(kTB[1:], o, n))
                elif ci < 17:
                    k.op('act', lambda e, o=o, n=n, bank=bank, sidx=sidx: e.copy(
                        out=st[sidx][:, 3 + o:3 + o + n], in_=psb(bank, n)), reads=[PB[bank]], writes=[stB[sidx]])
                else:
                    k.op('act', lambda e, ci=ci, o=o, n=n, bank=bank: e.activation(
                        out=zsT[:, ci - 17, o:o + n], in_=psb(bank, n), func=AF.Silu),
                        reads=[PB[bank]], writes=tb(zsB, o, n))
            if 5 <= ci < 17 and 'C' not in DBG:
                for t, si in special.items():
                    bank = 6 + (gi % 2)
                    gi += 1
                    for c in range(8):
                        k.op('pe', lambda e, c=c, t=t, wv=wv, bank=bank: e.matmul(
                            psb(bank, 128), lhsT=hT[:, c, t * 128:(t + 1) * 128], rhs=wv[:, c, :], start=(c == 0), stop=(c == 7)),
                            reads=[hTB[t], wblkB[s]], writes=[PB[bank]])
                    k.op('dve', lambda e, si=si, ch=ch, bank=bank: e.tensor_copy(
                        out=cvout[:, si, ch * 128:(ch + 1) * 128], in_=psb(bank, 128)), reads=[PB[bank]], writes=[cvoB[si]])
                sb_, ab_ = st[sidx], acc[sidx]
                k.op('pool', lambda e, sb_=sb_, ch=ch: e.tensor_copy(out=sb_[:, 0:3], in_=convhist[:, ch, :]),
                     reads=[convhistB], writes=[stB[sidx]])
                k.op('dve', lambda e, sb_=sb_, ab_=ab_, ch=ch: e.tensor_scalar(
                    out=ab_[:, 0:Tp], in0=sb_[:, 3:3 + Tp], scalar1=convw[:, ch, 3:4], scalar2=None, op0=ALU.mult),
                    reads=[stB[sidx], convwB], writes=[accB[sidx]])
                for j in range(3):
                    k.op('dve', lambda e, sb_=sb_, ab_=ab_, ch=ch, j=j: e.scalar_tensor_tensor(
                        out=ab_[:, 0:Tp], in0=sb_[:, j:j + Tp], scalar=convw[:, ch, j:j + 1], in1=ab_[:, 0:Tp],
                        op0=ALU.mult, op1=ALU.add), reads=[stB[sidx], convwB, accB[sidx]], writes=[accB[sidx]])
                if has_s:
                    k.op('pool', lambda e, sb_=sb_: e.tensor_copy(
                        out=ss[:, :, 3:11], in_=sb_[:, 3 + Tp:3 + Tp + 128].rearrange("p (b t) -> p b t", t=8)),
                        reads=[stB[sidx]], writes=[ssB])
                    k.op('pool', lambda e, ch=ch: e.tensor_copy(
                        out=ss[:, :, 0:3], in_=scT[:, ch, :].rearrange("p (b t) -> p b t", t=3)),
                        reads=[scTB], writes=[ssB])
                    av = ab_[:, Tp:Tp + 128].rearrange("p (b t) -> p b t", t=8)
                    k.op('dve', lambda e, av=av, ch=ch: e.tensor_scalar(
                        out=av, in0=ss[:, :, 3:11], scalar1=convw[:, ch, 3:4], scalar2=None, op0=ALU.mult),
                        reads=[ssB, convwB], writes=[accB[sidx]])
                    for j in range(3):
                        k.op('dve', lambda e, av=av, ch=ch, j=j: e.scalar_tensor_tensor(
                            out=av, in0=ss[:, :, j:j + 8], scalar=convw[:, ch, j:j + 1], in1=av,
                            op0=ALU.mult, op1=ALU.add), reads=[ssB, convwB, accB[sidx]], writes=[accB[sidx]])
                k.op('pool', lambda e, sb_=sb_, ch=ch: e.tensor_copy(out=convhist[:, ch, :], in_=sb_[:, Tp:Tp + 3]),
                     reads=[stB[sidx]], writes=[convhistB])
                typ, h = ch // 4, ch % 4
                sil, silB, sqb, sqbB, rn, rnB = sil2[sidx], silB2[sidx], sqb2[sidx], sqbB2[sidx], rn2[sidx], rnB2[sidx]
                if typ == 2:
                    k.op('act', lambda e, ab_=ab_, h=h, sil=sil, sqb=sqb, rn=rn: e.activation(
                        out=KQV[:, 0:nt, h, 2, :], in_=ab_[:, 0:T].rearrange("p (t n) -> p t n", n=128), func=AF.Silu),
                        reads=[accB[sidx]], writes=KQVB[0:nt])
                else:
                    k.op('act', lambda e, ab_=ab_, sil=sil, sqb=sqb, rn=rn: e.activation(out=sil[:, 0:T], in_=ab_[:, 0:T], func=AF.Silu),
                         reads=[accB[sidx]], writes=[silB])
                    k.op('pool', lambda e, sil=sil, sqb=sqb, rn=rn: e.tensor_tensor(out=sqb[:, 0:T], in0=sil[:, 0:T], in1=sil[:, 0:T], op=ALU.mult),
                         reads=[silB], writes=[sqbB])
                    for (o, n) in groups:
                        bank = gi % 6
                        gi += 1
                        k.op('pe', lambda e, o=o, n=n, bank=bank, sil=sil, sqb=sqb, rn=rn: e.matmul(
                            psb(bank, n), lhsT=Cb('ones'), rhs=sqb[:, o:o + n], start=True, stop=True),
                            reads=[sqbB, cbB], writes=[PB[bank]])
                        k.op('act', lambda e, o=o, n=n, bank=bank, typ=typ, sil=sil, sqb=sqb, rn=rn: e.activation(
                            out=rn[:, o:o + n], in_=psb(bank, n), func=AF.Ln, bias=EPS * (128.0 if typ == 0 else 1.0),
                            scale=(128.0 if typ == 0 else 1.0)), reads=[PB[bank]], writes=[rnB], n=n)
                    k.op('act', lambda e, sil=sil, sqb=sqb, rn=rn: e.activation(out=rn[:, 0:T], in_=rn[:, 0:T], func=AF.Exp, scale=-0.5),
                         reads=[rnB], writes=[rnB], n=T)
                    slot = 1 if typ == 0 else 0
                    k.op('dve', lambda e, h=h, slot=slot, sil=sil, sqb=sqb, rn=rn: e.tensor_tensor(
                        out=KQV[:, 0:nt, h, slot, :], in0=sil[:, 0:T].rearrange("p (t n) -> p t n", n=128),
                        in1=rn[:, 0:T].rearrange("p (t n) -> p t n", n=128), op=ALU.mult),
                        reads=[silB, rnB], writes=KQVB[0:nt], n=T)
        for t, si in (special.items() if 'O' not in DBG else []):
            if si == 0:
                k.dma('sp', nk_p[:, :], kvout[:, 0, 0:128], reads=[kvoB[0]], writes=[outB])
                k.dma('sp', nv_p[:, :], kvout[:, 0, 128:256], reads=[kvoB[0]], writes=[outB])
                k.dma('sp', ncv_p[:, :], cvout[125:128, 0, :], reads=[cvoB[0]], writes=[outB])
            else:
                k.dma('sp', nk_s[:, 0:120, :], cache_k[:, 8:128, :], writes=[outB])
                k.dma('sp', nv_s[:, 0:120, :], cache_v[:, 8:128, :], writes=[outB])
                for b in range(16):
                    q_ = 'sp' if b % 2 == 0 else 'act'
                    k.dma(q_, nk_s[b, 120:128, :], kvout[8 * b:8 * b + 8, 1, 0:128], reads=[kvoB[1]], writes=[outB2[b % 2]],
                          nbytes=4096)
                    k.dma(q_, nv_s[b, 120:128, :], kvout[8 * b:8 * b + 8, 1, 128:256], reads=[kvoB[1]], writes=[outB2[b % 2]],
                          nbytes=4096)
                    k.dma(q_, ncv_s[b, :, :], cvout[8 * b + 5:8 * b + 8, 1, :], reads=[cvoB[1]], writes=[outB2[b % 2]],
                          nbytes=18432)
        k.barrier()
        A.release(m1)
        if stop_after == 'm1':
            A.release(m0)
            return

        m2 = A.mark()
        Pc = A.alloc([2, 512], BF16)
        Pp = A.alloc([2, 512], BF16)
        PcB, PpB = [Buf(), Buf()], [Buf(), Buf()]
        rd = A.alloc([2, 512], F32)
        rdB = [Buf(), Buf()]
        if has_s:
            Sst = A.alloc([16, 128], F32)
            ckcv = Sst.rearrange("p b n -> p (b n)").bitcast(BF16).rearrange("p (b x) -> p b x", x=256)
            ck = ckcv[:, :, 0:128]
            cv = ckcv[:, :, 128:256]
            kcT = A.alloc([16, 128], BF16)
            kcTB = Buf()
            ckB2, cvB2 = [Buf(), Buf()], [Buf(), Buf()]
            for hf in range(2):
                k.dma('pool', ck[:, hf * 8:(hf + 1) * 8, :], cache_k[hf * 8:(hf + 1) * 8].rearrange("b s d -> s b d"),
                      writes=[ckB2[hf]], nbytes=1 << 19)
                k.dma('pool', cv[:, hf * 8:(hf + 1) * 8, :], cache_v[hf * 8:(hf + 1) * 8].rearrange("b s d -> s b d"),
                      writes=[cvB2[hf]], nbytes=1 << 19)
            Ssb = kcT
            Zk = A.alloc([16, 128], BF16)
            kdm = A.alloc([16, 128], BF16)
            SstB, SsbB, ZkB, kdmB = Buf(), kcTB, Buf(), Buf()
            k.op('pool', lambda e: e.memset(Zk, 0.0), writes=[ZkB], n=2048)
        Ug = A.alloc([4, 128], F32)
        E2m = A.alloc([4, 128], F32)
        D2s = A.alloc([4, 128], F32)
        EG = A.alloc([4, 128], F32)
        D2b = A.alloc([4, 128], F32)
        D2i = A.alloc([4, 128], F32)
        kdsc = A.alloc([4], F32)
        DT_INV = F32 if 'B' not in DBG else BF16
        Nn = A.alloc([4, 128], DT_INV)
        Nt = A.alloc([4, 128], DT_INV)
        Pw = [A.alloc([4, 128], DT_INV) for _ in range(2)]
        Ptw = [A.alloc([4, 128], DT_INV) for _ in range(2)]
        Yw = [A.alloc([4, 128], DT_INV) for _ in range(2)]
        Xs_dt = DT_INV
        QKd = A.alloc([4, 128], BF16)
        qd = A.alloc([4, 128], BF16)
        kdn = A.alloc([4, 128], BF16)
        kdec = A.alloc([4, 128], BF16)
        Xs = A.alloc([4, 128], Xs_dt)
        vnew = A.alloc([4, 128], BF16)
        tmpS = A.alloc([4, 128], F32)
        osq = A.alloc([4, 128], BF16)
        orn = A.alloc([4, 128], F32)
        otmp = A.alloc([4, 128], F32)
        (UgB, E2mB, D2sB, EGB, D2bB, D2iB, kdscB, NnB, NtB, QKdB, qdB, kdnB, kdecB, XsB, vnewB, tmpSB, osqB, ornB,
         otmpB) = [Buf() for _ in range(19)]
        MIXP = 1 if 'X' in DBG else 0
        if MIXP:
            Pb = [A.alloc([4, 128], BF16) for _ in range(2)]
            Ptb = [A.alloc([4, 128], BF16) for _ in range(2)]
            Yb = A.alloc([4, 128], BF16)
        else:
            Pb = Ptb = [None, None]
            Yb = None
        PbB, PtbB, YbB = [Buf(), Buf()], [Buf(), Buf()], Buf()
        NPAR = 1 if has_s else 2
        DB = [(EG, EGB, QKd, QKdB, qd, qdB, kdn, kdnB, kdec, kdecB, Yw, [Buf(), Buf()])]
        if NPAR == 2:
            DB.append((A.alloc([4, 128], F32), Buf(), A.alloc([4, 128], BF16), Buf(), A.alloc([4, 128], BF16), Buf(),
                       A.alloc([4, 128], BF16), Buf(), A.alloc([4, 128], BF16), Buf(),
                       [A.alloc([4, 128], DT_INV) for _ in range(2)], [Buf(), Buf()]))
        PwB = [Buf(), Buf()]
        PtwB = [Buf(), Buf()]
        YwB = [Buf(), Buf()]

        def bank4(b):
            return ps[:, b, :].rearrange("p (c n) -> p c n", n=128)

        A0, A1 = 0, 1

        def attn_group_scores(t, g, bank, kcol, Pbuf, PB_, mask):
            for c in range(4):
                k.op('pe', lambda e, c=c: e.matmul(
                    ps[:, bank, c * 128:(c + 1) * 128], lhsT=kT[g * 64:(g + 1) * 64, kcol:kcol + 128],
                    rhs=qT[g * 64:(g + 1) * 64, c, t * 128:(t + 1) * 128], start=True, stop=True),
                    reads=[kTB[kcol // 128], qTB[t]], writes=[PB[bank]], n=128)
            k.op('act', lambda e: e.activation(out=Pbuf[:, g, :], in_=psb(bank), func=AF.Exp),
                 reads=[PB[bank]], writes=[PB_[g]], n=512)
            k.op('pool', lambda e: e.tensor_tensor(
                out=Pbuf[:, g, :].rearrange("p (c n) -> p c n", n=128), in0=Pbuf[:, g, :].rearrange("p (c n) -> p c n", n=128),
                in1=bc(Cb(mask).unsqueeze(1), [128, 4, 128]), op=ALU.mult), reads=[PB_[g], cbB], writes=[PB_[g]], n=512)

        def attn_group_norm(t, g):
            k.op('dve', lambda e: e.tensor_tensor(
                out=rd[:, g, :].rearrange("p (c n) -> p c n", n=128), in0=bank4(A0),
                in1=bc(esink[:, g * 4:(g + 1) * 4].unsqueeze(2), [128, 4, 128]), op=ALU.add),
                reads=[PB[A0], derivB], writes=[rdB[g]], n=512)
            k.op('act', lambda e: e.activation(out=rd[:, g, :], in_=rd[:, g, :], func=AF.Ln), reads=[rdB[g]], writes=[rdB[g]], n=512)
            k.op('act', lambda e: e.activation(out=rd[:, g, :], in_=rd[:, g, :], func=AF.Exp, scale=-1.0),
                 reads=[rdB[g]], writes=[rdB[g]], n=512)
            k.op('dve', lambda e: e.tensor_tensor(
                out=oaT[g * 64:(g + 1) * 64, :, t * 128:(t + 1) * 128], in0=bank4(A1)[g * 64:(g + 1) * 64],
                in1=rd[g * 64:(g + 1) * 64, g, :].rearrange("p (c n) -> p c n", n=128), op=ALU.mult),
                reads=[PB[A1], rdB[g]], writes=[oaB[t]], n=512)

        def attn_prompt(t):
            has_prev = (t0 + t) > 0
            for g in range(2):
                attn_group_scores(t, g, A0, 128 + t * 128, Pc, PcB, 'M_cur')
                if has_prev:
                    attn_group_scores(t, g, A1, t * 128, Pp, PpB, 'M_prev')
                k.op('pe', lambda e, g=g: e.matmul(psb(A0), lhsT=Cb('ones'), rhs=Pc[:, g, :], start=True,
                                                   stop=not has_prev), reads=[PcB[g], cbB], writes=[PB[A0]], n=512)
                if has_prev:
                    k.op('pe', lambda e, g=g: e.matmul(psb(A0), lhsT=Cb('ones'), rhs=Pp[:, g, :], start=False, stop=True),
                         reads=[PpB[g], cbB], writes=[PB[A0]], n=512)
                for c in range(4):
                    k.op('pe', lambda e, g=g, c=c: e.matmul(
                        ps[:, A1, c * 128:(c + 1) * 128], lhsT=KV[:, 1 + t, 128:256], rhs=Pc[:, g, c * 128:(c + 1) * 128],
                        start=True, stop=not has_prev), reads=[KVB[1 + t], PcB[g]], writes=[PB[A1]], n=128)
                    if has_prev:
                        k.op('pe', lambda e, g=g, c=c: e.matmul(
                            ps[:, A1, c * 128:(c + 1) * 128], lhsT=KV[:, t, 128:256], rhs=Pp[:, g, c * 128:(c + 1) * 128],
                            start=False, stop=True), reads=[KVB[t], PpB[g]], writes=[PB[A1]], n=128)
                attn_group_norm(t, g)

        def attn_sample(t):
            for half in range(2):
                for b8 in range(8):
                    b = half * 8 + b8
                    k.op('pe', lambda e, b=b, b8=b8, half=half: e.transpose(
                        out=psb16(2 + half)[:, b8 * 128:(b8 + 1) * 128], in_=ck[:, b, :], identity=Cb('ident')),
                        reads=[ckB2[half], cbB], writes=[PB[2 + half]], n=128)
                k.op('act', lambda e, half=half: e.copy(
                    out=kcT[:, half * 8:(half + 1) * 8, :], in_=psb16(2 + half).rearrange("p (b n) -> p b n", n=128)),
                    reads=[PB[2 + half]], writes=[kcTB], n=1024)
            for g in range(2):
                attn_group_scores(t, g, A0, 128 + t * 128, Pc, PcB, 'M_blk')
                for b in range(16):
                    for c in range(4):
                        k.op('pe', lambda e, g=g, b=b, c=c: e.matmul(
                            ps[:, A1, c * 128 + 8 * b:c * 128 + 8 * b + 8],
                            lhsT=kcT[g * 64:(g + 1) * 64, b, :],
                            rhs=qT[g * 64:(g + 1) * 64, c, t * 128 + 8 * b:t * 128 + 8 * b + 8], start=True, stop=True),
                            reads=[kcTB, qTB[t]], writes=[PB[A1]], n=64)
                k.op('act', lambda e, g=g: e.activation(out=Pp[:, g, :], in_=psb(A1), func=AF.Exp),
                     reads=[PB[A1]], writes=[PpB[g]], n=512)
                k.op('pool', lambda e, g=g: e.tensor_tensor(
                    out=Pp[:, g, :].rearrange("p (x t) -> p x t", t=8), in0=Pp[:, g, :].rearrange("p (x t) -> p x t", t=8),
                    in1=bc(Cb('M_cache')[:, 0:8].unsqueeze(1), [128, 64, 8]), op=ALU.mult),
                    reads=[PpB[g], cbB], writes=[PpB[g]], n=512)
                k.op('pe', lambda e, g=g: e.matmul(psb(A0), lhsT=Cb('ones'), rhs=Pc[:, g, :], start=True, stop=False),
                     reads=[PcB[g], cbB], writes=[PB[A0]], n=512)
                k.op('pe', lambda e, g=g: e.matmul(psb(A0), lhsT=Cb('ones'), rhs=Pp[:, g, :], start=False, stop=True),
                     reads=[PpB[g], cbB], writes=[PB[A0]], n=512)
                for c in range(4):
                    k.op('pe', lambda e, g=g, c=c: e.matmul(
                        ps[:, A1, c * 128:(c + 1) * 128], lhsT=KV[:, 1 + t, 128:256], rhs=Pc[:, g, c * 128:(c + 1) * 128],
                        start=True, stop=False), reads=[KVB[1 + t], PcB[g]], writes=[PB[A1]], n=128)
                    for b in range(16):
                        k.op('pe', lambda e, g=g, c=c, b=b: e.matmul(
                            ps[:, A1, c * 128 + 8 * b:c * 128 + 8 * b + 8], lhsT=cv[:, b, :],
                            rhs=Pp[:, g, c * 128 + 8 * b:c * 128 + 8 * b + 8], start=False, stop=(b == 15)),
                            reads=[cvB2[b // 8], PpB[g]], writes=[PB[A1]], n=64)
                attn_group_norm(t, g)

        def dn_prep(t, sample, p):
            EG, EGB, QKd, QKdB, qd, qdB, kdn, kdnB, kdec, kdecB, Yw, YwB = DB[p]
            sfx = '_s' if sample else '_p'
            gt_ = gcol[:, t, :]
            bt_ = betac[:, t, :]
            k.op('dve', lambda e: e.tensor_tensor(out=Ug, in0=bc(C('U' + sfx).unsqueeze(1), [128, 4, 128]),
                                                  in1=bc(gt_.unsqueeze(2), [128, 4, 128]), op=ALU.mult),
                 reads=[cfB, gB], writes=[UgB])
            Ugf = Ug.rearrange("p h n -> p (h n)")
            k.op('pe', lambda e: e.matmul(psb(2), lhsT=C('SL' + sfx), rhs=Ugf, start=True, stop=True),
                 reads=[UgB, cfB], writes=[PB[2]])
            k.op('pe', lambda e: e.matmul(psb(3), lhsT=C('ones'), rhs=Ugf, start=True, stop=True),
                 reads=[UgB, cfB], writes=[PB[3]])
            k.op('dve', lambda e: e.tensor_tensor(out=E2m, in0=bank4(2), in1=bc(C('MB' + sfx).unsqueeze(1), [128, 4, 128]),
                                                  op=ALU.add), reads=[PB[2], cfB], writes=[E2mB])
            k.op('pe', lambda e: e.matmul(psb(2, 4), lhsT=C('SL' + sfx), rhs=gt_, start=True, stop=True),
                 reads=[gB, cfB], writes=[PB[2]])
            k.op('act', lambda e: e.activation(out=D2s, in_=E2m, func=AF.Exp), reads=[E2mB], writes=[D2sB])
            k.op('act', lambda e: e.activation(out=EG, in_=bank4(3), func=AF.Exp), reads=[PB[3]], writes=[EGB])
            k.op('act', lambda e: e.activation(out=kdsc, in_=psb(2, 4), func=AF.Exp), reads=[PB[2]], writes=[kdscB])
            k.op('pool', lambda e: e.tensor_tensor(out=D2b, in0=D2s, in1=bc(bt_.unsqueeze(2), [128, 4, 128]), op=ALU.mult),
                 reads=[D2sB, gB], writes=[D2bB])
            k.op('pool', lambda e: e.tensor_tensor(out=D2i, in0=D2s, in1=bc(C('ident').unsqueeze(1), [128, 4, 128]),
                                                   op=ALU.add), reads=[D2sB, cfB], writes=[D2iB])
            for h in range(4):
                k.op('pe', lambda e, h=h: e.matmul(
                    ps[:, 4 + h // 2, (h % 2) * 256:(h % 2) * 256 + 256], lhsT=KQV[:, t, h, 0, :],
                    rhs=KQV[:, t, h, 0:2, :], start=True, stop=True), reads=[KQVB[t]], writes=[PB[4 + h // 2]])
            GQ = ps[:, 4:6, :].rearrange("p b (x w n) -> p (b x) w n", x=2, w=2, n=128)
            k.op('dve', lambda e: e.tensor_tensor(out=Nn, in0=GQ[:, :, 0, :], in1=D2b, op=ALU.mult),
                 reads=[PB[4], PB[5], D2bB], writes=[NnB])
            k.op('dve', lambda e: e.tensor_tensor(out=QKd, in0=GQ[:, :, 1, :], in1=D2i, op=ALU.mult),
                 reads=[PB[4], PB[5], D2iB], writes=[QKdB])
            b16 = psb16(2).rearrange("p (c n) -> p c n", n=128)
            if DT_INV == F32:
                for h in range(4):
                    k.op('pe', lambda e, h=h: e.transpose(out=ps[:, 3, h * 128:(h + 1) * 128], in_=Nn[:, h, :], identity=C('ident')),
                         reads=[NnB, cfB], writes=[PB[3]])
                k.op('act', lambda e: e.copy(out=Nt, in_=bank4(3)), reads=[PB[3]], writes=[NtB])
            else:
                for h in range(4):
                    k.op('pe', lambda e, h=h: e.transpose(out=b16[:, h, :], in_=Nn[:, h, :], identity=Cb('ident')),
                         reads=[NnB, cbB], writes=[PB[2]])
                k.op('act', lambda e: e.copy(out=Nt, in_=b16[:, 0:4, :]), reads=[PB[2]], writes=[NtB])
            for h in range(4):
                k.op('pe', lambda e, h=h: e.transpose(out=b16[:, 4 + h, :], in_=KQV[:, t, h, 0, :], identity=Cb('ident')),
                     reads=[KQVB[t], cbB], writes=[PB[2]])
            k.op('dve', lambda e: e.tensor_tensor(out=kdec, in0=b16[:, 4:8, :], in1=bc(kdsc.unsqueeze(2), [128, 4, 128]),
                                                  op=ALU.mult), reads=[PB[2], kdscB], writes=[kdecB])
            k.op('pool', lambda e: e.tensor_tensor(out=Yw[0], in0=bc(Cb('ident').unsqueeze(1), [128, 4, 128]), in1=Nn,
                                                   op=ALU.subtract), reads=[NnB, cbB], writes=[YwB[0]])
            Pcur, PcurB, Ptcur, PtcurB = Nn, NnB, Nt, NtB
            yi = 0
            NLEV = 3 if sample else 6
            for lvl in range(1, NLEV + 1):
                w = lvl % 2
                for h in range(4):
                    k.op('pe', lambda e, h=h, Pcur=Pcur, Ptcur=Ptcur: e.matmul(
                        ps[:, 3, h * 128:(h + 1) * 128], lhsT=Pcur[:, h, :], rhs=Ptcur[:, h, :], start=True, stop=True),
                        reads=[PcurB, PtcurB], writes=[PB[3]], n=128, c=(0.27 if (lvl <= 2 or not MIXP) else 0.11))
                if lvl < NLEV:
                    for h in range(4):
                        k.op('pe', lambda e, h=h, Pcur=Pcur, Ptcur=Ptcur: e.matmul(
                            ps[:, 4, h * 128:(h + 1) * 128], lhsT=Ptcur[:, h, :], rhs=Pcur[:, h, :], start=True, stop=True),
                            reads=[PcurB, PtcurB], writes=[PB[4]], n=128, c=(0.27 if (lvl <= 2 or not MIXP) else 0.11))
                if lvl < 2 or MIXP == 0:
                    dPt, dPtB, dP, dPB = Ptw[w], PtwB[w], Pw[w], PwB[w]
                else:
                    dPt, dPtB, dP, dPB = Ptb[w], PtbB[w], Pb[w], PbB[w]
                k.op('act', lambda e, dPt=dPt: e.copy(out=dPt, in_=bank4(3)), reads=[PB[3]], writes=[dPtB], n=512)
                if lvl < NLEV:
                    k.op('dve', lambda e, dP=dP: e.tensor_copy(out=dP, in_=bank4(4)), reads=[PB[4]], writes=[dPB], n=512)
                if lvl < 2 or MIXP == 0:
                    yr, yrB = Yw[yi], YwB[yi]
                else:
                    yr, yrB = Yb, YbB
                for h in range(4):
                    k.op('pe', lambda e, h=h, dPt=dPt, yr=yr: e.matmul(
                        ps[:, 5, h * 128:(h + 1) * 128], lhsT=dPt[:, h, :], rhs=yr[:, h, :], start=True, stop=True),
                        reads=[dPtB, yrB], writes=[PB[5]], n=128, c=(0.27 if (lvl < 2 or not MIXP) else 0.11))
                k.op('dve', lambda e, yi=yi: e.tensor_tensor(out=Yw[1 - yi], in0=bank4(5), in1=Yw[yi], op=ALU.add),
                     reads=[PB[5], YwB[yi]], writes=[YwB[1 - yi]], n=512)
                yi = 1 - yi
                if lvl < NLEV and MIXP:
                    k.op('act', lambda e, yi=yi: e.copy(out=Yb, in_=Yw[yi]), reads=[YwB[yi]], writes=[YbB], n=512)
                Pcur, PcurB, Ptcur, PtcurB = dP, dPB, dPt, dPtB
            k.op('dve', lambda e: e.tensor_tensor(out=qd, in0=KQV[:, t, :, 1, :], in1=EG, op=ALU.mult),
                 reads=[KQVB[t], EGB], writes=[qdB])
            k.op('dve', lambda e: e.scalar_tensor_tensor(out=kdn, in0=KQV[:, t, :, 0, :], scalar=-1.0, in1=EG,
                                                         op0=ALU.mult, op1=ALU.mult), reads=[KQVB[t], EGB], writes=[kdnB])
            return yi

        def dn_out(t):
            k.op('act', lambda e: e.activation(out=osq, in_=bank4(6), func=AF.Square), reads=[PB[6]], writes=[osqB])
            k.op('pe', lambda e: e.matmul(psb(7), lhsT=Cb('ones'), rhs=osq.rearrange("p h n -> p (h n)"), start=True, stop=True),
                 reads=[osqB, cbB], writes=[PB[7]])
            k.op('act', lambda e: e.activation(out=orn, in_=bank4(7), func=AF.Ln, bias=EPS, scale=1.0 / 128),
                 reads=[PB[7]], writes=[ornB])
            k.op('act', lambda e: e.activation(out=orn, in_=orn, func=AF.Exp, scale=-0.5), reads=[ornB], writes=[ornB])
            k.op('dve', lambda e: e.tensor_tensor(out=otmp, in0=bank4(6), in1=orn, op=ALU.mult),
                 reads=[PB[6], ornB], writes=[otmpB])
            k.op('dve', lambda e: e.scalar_tensor_tensor(
                out=obT[:, :, t * 128:(t + 1) * 128], in0=otmp, scalar=smallp[:, 16:17],
                in1=zsT[:, :, t * 128:(t + 1) * 128], op0=ALU.mult, op1=ALU.mult),
                reads=[otmpB, zsB[t], smallpB], writes=[obB[t]])

        def dn_scan_prompt(t, wi, p):
            EG, EGB, QKd, QKdB, qd, qdB, kdn, kdnB, kdec, kdecB, Yw, YwB = DB[p]
            WT, WTB = Yw[wi], YwB[wi]
            bt_ = betac[:, t, :]
            for h in range(4):
                k.op('pe', lambda e, h=h: e.matmul(ps[:, 6, h * 128:(h + 1) * 128], lhsT=KQV[:, t, h, 2, :], rhs=Cb('ident'),
                                                   start=True, stop=False), reads=[KQVB[t], cbB], writes=[PB[6]])
                k.op('pe', lambda e, h=h: e.matmul(ps[:, 6, h * 128:(h + 1) * 128], lhsT=kdn[:, h, :], rhs=Sbf[:, h, :],
                                                   start=False, stop=True), reads=[kdnB, SbfB], writes=[PB[6]])
            k.op('act', lambda e: e.copy(out=Xs, in_=bank4(6)), reads=[PB[6]], writes=[XsB])
            for h in range(4):
                k.op('pe', lambda e, h=h: e.matmul(ps[:, 7, h * 128:(h + 1) * 128], lhsT=WT[:, h, :], rhs=Xs[:, h, :],
                                                   start=True, stop=True), reads=[WTB, XsB], writes=[PB[7]])
            k.op('dve', lambda e: e.tensor_tensor(out=vnew, in0=bank4(7), in1=bc(bt_.unsqueeze(2), [128, 4, 128]),
                                                  op=ALU.mult), reads=[PB[7], gB], writes=[vnewB])
            for h in range(4):
                k.op('pe', lambda e, h=h: e.matmul(ps[:, 6, h * 128:(h + 1) * 128], lhsT=Sbf[:, h, :], rhs=qd[:, h, :],
                                                   start=True, stop=False), reads=[SbfB, qdB], writes=[PB[6]])
                k.op('pe', lambda e, h=h: e.matmul(ps[:, 6, h * 128:(h + 1) * 128], lhsT=vnew[:, h, :], rhs=QKd[:, h, :],
                                                   start=False, stop=True), reads=[vnewB, QKdB], writes=[PB[6]])
            for h in range(4):
                k.op('pe', lambda e, h=h: e.matmul(ps[:, 7, h * 128:(h + 1) * 128], lhsT=kdec[:, h, :], rhs=vnew[:, h, :],
                                                   start=True, stop=True), reads=[kdecB, vnewB], writes=[PB[7]])
            k.op('dve', lambda e: e.tensor_tensor(out=tmpS, in0=S, in1=bc(EG[:, :, 127:128], [128, 4, 128]), op=ALU.mult),
                 reads=[SB, EGB], writes=[tmpSB])
            k.op('dve', lambda e: e.tensor_tensor(out=S, in0=bank4(7), in1=tmpS, op=ALU.add),
                 reads=[PB[7], tmpSB], writes=[SB])
            k.op('act', lambda e: e.copy(out=Sbf, in_=S), reads=[SB], writes=[SbfB])
            dn_out(t)

        def dn_scan_sample(t, wi, p):
            EG, EGB, QKd, QKdB, qd, qdB, kdn, kdnB, kdec, kdecB, Yw, YwB = DB[p]
            WT, WTB = Yw[wi], YwB[wi]
            bt_ = betac[:, t, :]
            SstH = [Buf(), Buf()]
            SsbH = [Buf(), Buf()]
            for h in range(4):
                for hf in range(2):
                    bs = slice(hf * 8, (hf + 1) * 8)
                    extra = [ckB2[hf], cvB2[hf]] if h == 0 else []
                    k.dma('sp', Sst[:, bs, :], st_delta[bs, h].rearrange("b d e -> d b e"), writes=[SstH[hf]] + extra,
                          nbytes=1 << 19)
                    k.op('act', lambda e, bs=bs: e.copy(out=Ssb[:, bs, :], in_=Sst[:, bs, :]), reads=[SstH[hf]],
                         writes=[SsbH[hf]] + ([kcTB] if h == 0 else []), n=1024)
                Zf = Zk.rearrange("p b i -> p (b i)")
                k.op('dve', lambda e, h=h, Zf=Zf: e.tensor_copy(
                    out=Zf[:, 0:2040].rearrange("p (b r) -> p b r", r=136)[:, :, 0:8],
                    in_=kdn[:, h, 0:120].rearrange("p (b t) -> p b t", t=8)), reads=[kdnB], writes=[ZkB], n=128)
                k.op('dve', lambda e, h=h, Zf=Zf: e.tensor_copy(out=Zf[:, 2040:2048], in_=kdn[:, h, 120:128]),
                     reads=[kdnB], writes=[ZkB], n=64)
                k.op('pool', lambda e, h=h: e.tensor_tensor(
                    out=kdm, in0=bc(kdec[:, h, :].unsqueeze(1), [128, 16, 128]),
                    in1=bc(Cb('BM')[:, 0:16].unsqueeze(2), [128, 16, 128]), op=ALU.mult),
                    reads=[kdecB, cbB], writes=[kdmB], n=2048)
                k.op('pe', lambda e, h=h: e.matmul(ps[:, 6, h * 128:(h + 1) * 128], lhsT=KQV[:, t, h, 2, :], rhs=Cb('ident'),
                                                   start=True, stop=False), reads=[KQVB[t], cbB], writes=[PB[6]], n=128)
                for b in range(16):
                    k.op('pe', lambda e, h=h, b=b: e.matmul(ps[:, 6, h * 128:(h + 1) * 128], lhsT=Zk[:, b, :], rhs=Ssb[:, b, :],
                                                            start=False, stop=(b == 15)), reads=[ZkB, SsbH[b // 8]],
                         writes=[PB[6]], n=128)
                k.op('act', lambda e, h=h: e.copy(out=Xs[:, h, :], in_=ps[:, 6, h * 128:(h + 1) * 128]),
                     reads=[PB[6]], writes=[XsB], n=128)
                k.op('pe', lambda e, h=h: e.matmul(ps[:, 7, h * 128:(h + 1) * 128], lhsT=WT[:, h, :], rhs=Xs[:, h, :],
                                                   start=True, stop=True), reads=[WTB, XsB], writes=[PB[7]], n=128, c=0.27)
                k.op('dve', lambda e, h=h: e.tensor_scalar(out=vnew[:, h, :], in0=ps[:, 7, h * 128:(h + 1) * 128],
                                                           scalar1=bt_[:, h:h + 1], scalar2=None, op0=ALU.mult),
                     reads=[PB[7], gB], writes=[vnewB], n=128)
                for b in range(16):
                    k.op('pe', lambda e, h=h, b=b: e.matmul(
                        ps[:, 6, h * 128 + 8 * b:h * 128 + 8 * b + 8], lhsT=Ssb[:, b, :], rhs=qd[:, h, 8 * b:8 * b + 8],
                        start=(b == 0), stop=False), reads=[SsbH[b // 8], qdB], writes=[PB[6]], n=64)
                k.op('pe', lambda e, h=h: e.matmul(ps[:, 6, h * 128:(h + 1) * 128], lhsT=vnew[:, h, :], rhs=QKd[:, h, :],
                                                   start=False, stop=True), reads=[vnewB, QKdB], writes=[PB[6]], n=128)
                for b in range(16):
                    k.op('pe', lambda e, h=h, b=b: e.matmul(
                        ps[:, b // 4, (b % 4) * 128:(b % 4) * 128 + 128], lhsT=kdm[:, b, :], rhs=vnew[:, h, :],
                        start=True, stop=True), reads=[kdmB, vnewB], writes=[PB[b // 4]], n=128)
                for b in range(16):
                    k.op('dve', lambda e, h=h, b=b: e.scalar_tensor_tensor(
                        out=Sst[:, b, :], in0=Sst[:, b, :], scalar=EG[:, h, 8 * b + 7:8 * b + 8],
                        in1=ps[:, b // 4, (b % 4) * 128:(b % 4) * 128 + 128], op0=ALU.mult, op1=ALU.add),
                        reads=[SstH[b // 8], EGB, PB[b // 4]], writes=[SstH[b // 8]], n=128)
                for hf in range(2):
                    bs = slice(hf * 8, (hf + 1) * 8)
                    k.dma('sp', nd_s[bs, h].rearrange("b d e -> d b e"), Sst[:, bs, :], reads=[SstH[hf]], writes=[outB],
                          nbytes=1 << 19)
            dn_out(t)

        for t in range(nt):
            p = (t % 2) if NPAR == 2 else 0
            if t < npt:
                attn_prompt(t)
                wi = dn_prep(t, False, p)
                dn_scan_prompt(t, wi, p)
                if t0 + t == 15:
                    k.dma('sp', nd_p.rearrange("h d e -> d h e"), S, reads=[SB], writes=[outB])
            else:
                attn_sample(t)
                wi = dn_prep(t, True, p)
                dn_scan_sample(t, wi, p)
        if not has_s:
            k.op('pool', lambda e: e.tensor_copy(out=kT[:, 0:128], in_=kT[:, npt * 128:npt * 128 + 128]),
                 reads=[kTB[npt]], writes=[kTB[0]])
            k.op('pool', lambda e: e.tensor_copy(out=KV[:, 0, :], in_=KV[:, npt, :]), reads=[KVB[npt]], writes=[KVB[0]])
        for blk_ in range(pre3):
            m3_wload(blk_)
        k.barrier()
        A.release(m2)
        if stop_after == 'm2':
            A.release(m0)
            return

        m3 = A.mark()
        post = make_post()
        wo = A.alloc([8, D], BF16)
        woB = Buf()
        k.dma('pool', wo, w_o.rearrange("(c p) n -> p c n", p=128), writes=[woB])
        mT = A.alloc([8, NTMAX * 128], BF16)
        mTB = [Buf() for _ in range(8)]
        wga = [WS[0], WS[1]]
        wgb = [WS[2], WS[3]]
        wa = [WS[4][:, 0:4, :], WS[4][:, 4:8, :]]
        wb = [WS[5][:, 0:4, :], WS[5][:, 4:8, :]]
        sga = [A.alloc([512], F32) for _ in range(2)]
        sgb = [A.alloc([512], F32) for _ in range(2)]
        mm1 = [A.alloc([512], F32) for _ in range(2)]
        mm2 = [A.alloc([512], F32) for _ in range(2)]
        sgaB, sgbB, mm1B, mm2B = [[Buf(), Buf()] for _ in range(4)]
        gi = 0
        for blk in range(4):
            s = blk % 2
            if blk >= pre3:
                m3_wload(blk)
            for jj in range(2):
                c8 = blk * 2 + jj
                cs = slice(jj * 128, (jj + 1) * 128)
                for (o, n) in groups:
                    b0 = (gi % 2) * 4
                    x = gi % 2
                    gi += 1
                    for c in range(8):
                        k.op('pe', lambda e, c=c, s=s, cs=cs, o=o, n=n, b0=b0: e.matmul(
                            psb(b0, n), lhsT=wga[s][:, c, cs], rhs=hT[:, c, o:o + n], start=(c == 0), stop=(c == 7)),
                            reads=[WSB[s]] + tb(hTB, o, n), writes=[PB[b0]], n=n)
                    for c in range(8):
                        k.op('pe', lambda e, c=c, s=s, cs=cs, o=o, n=n, b0=b0: e.matmul(
                            psb(b0 + 1, n), lhsT=wgb[s][:, c, cs], rhs=hT[:, c, o:o + n], start=(c == 0), stop=(c == 7)),
                            reads=[WSB[2 + s]] + tb(hTB, o, n), writes=[PB[b0 + 1]], n=n)
                    for c in range(4):
                        k.op('pe', lambda e, c=c, s=s, cs=cs, o=o, n=n, b0=b0: e.matmul(
                            psb(b0 + 2, n), lhsT=wa[s][:, c, cs], rhs=oaT[:, c, o:o + n], start=(c == 0), stop=(c == 3)),
                            reads=[WSB[4]] + tb(oaB, o, n), writes=[PB[b0 + 2]], n=n)
                    for c in range(4):
                        k.op('pe', lambda e, c=c, s=s, cs=cs, o=o, n=n, b0=b0: e.matmul(
                            psb(b0 + 3, n), lhsT=wb[s][:, c, cs], rhs=obT[:, c, o:o + n], start=(c == 0), stop=(c == 3)),
                            reads=[WSB[5]] + tb(obB, o, n), writes=[PB[b0 + 3]], n=n)
                    k.op('act', lambda e, x=x, n=n, b0=b0: e.activation(out=sga[x][:, 0:n], in_=psb(b0, n), func=AF.Sigmoid),
                         reads=[PB[b0]], writes=[sgaB[x]])
                    k.op('act', lambda e, x=x, n=n, b0=b0: e.activation(out=sgb[x][:, 0:n], in_=psb(b0 + 1, n), func=AF.Sigmoid),
                         reads=[PB[b0 + 1]], writes=[sgbB[x]])
                    k.op('dve', lambda e, x=x, n=n, b0=b0: e.tensor_tensor(out=mm1[x][:, 0:n], in0=psb(b0 + 2, n),
                                                                           in1=sga[x][:, 0:n], op=ALU.mult),
                         reads=[PB[b0 + 2], sgaB[x]], writes=[mm1B[x]])
                    k.op('dve', lambda e, x=x, n=n, b0=b0: e.tensor_tensor(out=mm2[x][:, 0:n], in0=psb(b0 + 3, n),
                                                                           in1=sgb[x][:, 0:n], op=ALU.mult),
                         reads=[PB[b0 + 3], sgbB[x]], writes=[mm2B[x]])
                    k.op('pool', lambda e, x=x, n=n, o=o, c8=c8: e.tensor_tensor(out=mT[:, c8, o:o + n], in0=mm1[x][:, 0:n],
                                                                                 in1=mm2[x][:, 0:n], op=ALU.add),
                         reads=[mm1B[x], mm2B[x]], writes=[mTB[c8]])
        for t in range(nt):
            banks = [0, 1] if t % 2 == 0 else [2, 3]
            for hh in range(2):
                for c in range(8):
                    k.op('pe', lambda e, t=t, hh=hh, c=c, banks=banks: e.matmul(
                        psb(banks[hh]), lhsT=mT[:, c, t * 128:(t + 1) * 128], rhs=wo[:, c, hh * 512:(hh + 1) * 512],
                        start=(c == 0), stop=(c == 7)), reads=[mTB[c], woB], writes=[PB[banks[hh]]])
            post(t, banks, 1, False)
        for fb_ in range(next_ffn_pre):
            ffn_wload(1, fb_)
        A.release(m0)

    thirds = [(0, 6, False), (6, 6, False), (12, 4, True)]
    def load_x(t0, npt, has_s):
        for t in range(npt):
            k.dma('sp', X[:, t, :], x_p[(t0 + t) * 128:(t0 + t + 1) * 128, :], writes=[XB[t]], nbytes=1 << 19)
        if has_s:
            k.dma('sp', X[:, npt, :], x_s[:, :], writes=[XB[npt]], nbytes=1 << 19)

    load_x(*thirds[0])
    for ti, (t0, npt, has_s) in enumerate(thirds):
        nt = npt + (1 if has_s else 0)
        PF = 0 if 'P' in DBG else 1
        if not skip_ffn:
            ffn(nt, 0, pre=(2 * PF if t0 > 0 else 0))
        if stop_after != 'ffn1':
            for b_ in range(2 * PF):
                m1_wload(b_)
        k.barrier()
        if stop_after != 'ffn1':
            mixer(nt, npt, has_s, t0, pre1=2 * PF, pre3=PF, next_ffn_pre=(2 * PF if not skip_ffn else 0))
            k.barrier()
            if not skip_ffn:
                ffn(nt, 1, pre=2 * PF)
                if not has_s:
                    for fb_ in range(2 * PF):
                        ffn_wload(0, fb_)
            if skip_ffn:
                k.barrier()
        oB = Buf()
        for t in range(npt):
            k.dma('sp', y_p[(t0 + t) * 128:(t0 + t + 1) * 128, :], X[:, t, :], reads=[XB[t]], writes=[oB])
        if has_s:
            k.dma('sp', y_s[:, :], X[:, npt, :], reads=[XB[npt]], writes=[oB])
        if ti + 1 < len(thirds):
            load_x(*thirds[ti + 1])
        if skip_ffn or stop_after is not None or ti + 1 == len(thirds):
            k.barrier()
    k.barrier()
    k.emit()
    return nc


def prep_inputs(inp):
    f = lambda a: np.ascontiguousarray(np.asarray(a, dtype=np.float32))
    names, carr, cm = make_consts()
    w_in = f(inp['w_in'][0])
    offs = np.cumsum([0, 512, 128, 128, 1536, 512, 4, 4, 1024, 1024])
    qa, ka, va, qkvb, zb, ab, bb, ga, gb = [w_in[:, offs[i]:offs[i + 1]] for i in range(9)]
    perm = np.concatenate([np.r_[c * 64:(c + 1) * 64, (4 + c) * 64:(5 + c) * 64] for c in range(4)])
    w_fm = f(np.concatenate([qa[:, perm], ka, qkvb, zb, ga, gb], axis=1))
    w_tm = f(np.concatenate([ka, va, ab, bb, np.zeros((D, 56), np.float32)], axis=1))
    w_ba = f(inp['w_branch_a'][0][perm, :])
    gpost = f(np.broadcast_to(np.stack([inp['ffn1_norm_post'][0], inp['mix_norm_post'][0], inp['ffn2_norm_post'][0]])[None],
                              (128, 3, D)))
    gpre = np.stack([inp['ffn1_norm_pre'][0], inp['mix_norm_pre'][0], inp['ffn2_norm_pre'][0]])
    gpreT = f(gpre.reshape(3, 8, 128).transpose(2, 0, 1))
    smallp = np.zeros((128, 32), np.float32)
    smallp[:, 0:4] = np.asarray(inp['dn_a_log'][0])[None]
    smallp[:, 4:8] = np.asarray(inp['dn_dt_bias'][0])[None]
    smallp[:, 8:16] = np.asarray(inp['attn_sinks'][0])[None]
    smallp[:, 16] = np.asarray(inp['dn_out_norm'][0])
    convw = f(np.asarray(inp['conv_w'][0]).reshape(4, 12, 128).transpose(2, 1, 0))
    shared = {
        'consts': carr, 'colmask': cm, 'gpost': gpost, 'gpreT': gpreT, 'smallp': smallp, 'convw': convw,
        'w_up1': f(inp['ffn1_w_up'][0]), 'w_up2': f(inp['ffn2_w_up'][0]),
        'w_down1': f(inp['ffn1_w_down'][0]), 'w_down2': f(inp['ffn2_w_down'][0]),
        'w_fm': w_fm, 'w_tm': w_tm, 'w_ba': w_ba, 'w_bb': f(inp['w_branch_b'][0]), 'w_o': f(inp['w_out'][0]),
    }
    maps = []
    for c in range(NCORE):
        m = dict(shared)
        m['x_p'] = f(inp['x_prompt'][c])
        m['x_s'] = f(np.asarray(inp['x_sample'])[16 * c:16 * c + 16].reshape(128, D))
        m['cache_k'] = f(np.asarray(inp['cache_swa_k'])[0, 16 * c:16 * c + 16].reshape(16, 128, 128))
        m['cache_v'] = f(np.asarray(inp['cache_swa_v'])[0, 16 * c:16 * c + 16].reshape(16, 128, 128))
        m['st_conv'] = f(np.asarray(inp['state_conv'])[0, 16 * c:16 * c + 16].reshape(48, 1536))
        m['st_delta'] = f(np.asarray(inp['state_delta'])[0, 16 * c:16 * c + 16])
        maps.append(m)
    return maps


_NC_CACHE = {}


def kernel(**inputs):
    maps = prep_inputs(inputs)
    if 'nc' not in _NC_CACHE:
        _NC_CACHE['nc'] = build_nc()
    nc = _NC_CACHE['nc']
    res = run_bass_kernel_spmd(nc, maps, core_ids=list(range(NCORE)))
    R = res.results
    g = lambda n: np.stack([np.asarray(r[n], dtype=np.float32) for r in R])
    y_p = g('y_p')
    y_s = g('y_s').reshape(128, 8, D)
    nk_p = g('nk_p').reshape(1, 8, 128, 2, 64)
    nv_p = g('nv_p').reshape(1, 8, 128, 2, 64)
    ncv_p = g('ncv_p').reshape(1, 8, 3, 1536)
    nd_p = g('nd_p').reshape(1, 8, 4, 128, 128)
    nk_s = g('nk_s').reshape(1, 128, 128, 2, 64)
    nv_s = g('nv_s').reshape(1, 128, 128, 2, 64)
    ncv_s = g('ncv_s').reshape(1, 128, 3, 1536)
    nd_s = g('nd_s').reshape(1, 128, 4, 128, 128)
    return (y_p, y_s, nk_p, nv_p, ncv_p, nd_p, nk_s, nv_s, ncv_s, nd_s)
```

```python
import numpy as np
import os
DBG = os.environ.get('DBG', '')
import concourse.bass as bass
import concourse.mybir as mybir
from concourse.bass_utils import run_bass_kernel_spmd

F32 = mybir.dt.float32
BF16 = mybir.dt.bfloat16
AF = mybir.ActivationFunctionType
ALU = mybir.AluOpType

D = 1024
DFF = 2816
NCORE = 8
EPS = 1e-6


class Buf:
    __slots__ = ('name', 'w', 'r', 'excl')

    def __init__(self, name='', excl=False):
        self.name = name
        self.w = None
        self.r = []
        self.excl = excl


class KB:
    ENG = ['pe', 'act', 'dve', 'pool', 'sp']
    BASE = {'pe': 0.03, 'act': 0.22, 'dve': 0.12, 'pool': 0.25, 'sp': 0.05}
    PER = {'pe': 1.0 / 2400, 'act': 1.0 / 1100, 'dve': 1.0 / 900, 'pool': 1.0 / 500, 'sp': 0.0}

    def __init__(self, nc, n_dma_sems=32, schedule=True):
        self.nc = nc
        self.schedule = schedule
        self.prog = {e: [] for e in self.ENG}
        self.sems = {}
        self.cnt = {}
        for e in self.ENG:
            self.sems['e_' + e] = nc.alloc_semaphore('s_' + e)
            self.cnt['e_' + e] = 0
        self.dma_sems = []
        for i in range(n_dma_sems):
            kk = 'd_%d' % i
            self.sems[kk] = nc.alloc_semaphore('s_' + kk)
            self.cnt[kk] = 0
            self.dma_sems.append(kk)
        self.dma_rr = {'sp': 0, 'pool': 0, 'act': 0}
        self.seen = {e: {} for e in self.ENG}
        self.nodes = []

    def op(self, eng, fn, reads=(), writes=(), n=512, c=None):
        ex = [b for b in reads if b.excl]
        if ex:
            reads = [b for b in reads if not b.excl]
            writes = list(writes) + ex
        if c is None:
            c = self.BASE[eng] + n * self.PER[eng]
        self.nodes.append(['op', eng, fn, list(reads), list(writes), c, c + 0.15])

    def dma(self, q, out, in_, reads=(), writes=(), nbytes=1 << 20):
        busy = 1.0 if q == 'pool' else 0.08
        if q == 'act':
            busy = 0.1
        lat = 2.0 + nbytes / 150e3
        self.nodes.append(['dma', q, (out, in_), list(reads), list(writes), busy, busy + lat])

    def barrier(self):
        self.nodes.append(['bar'])

    def _schedule_segment(self, seg):
        nn = len(seg)
        if nn <= 2 or not self.schedule:
            return list(range(nn))
        succs = [[] for _ in range(nn)]
        npred = [0] * nn
        lastw = {}
        readers = {}
        for i, nd in enumerate(seg):
            deps = set()
            for b in nd[3]:
                w = lastw.get(id(b))
                if w is not None:
                    deps.add(w)
            for b in nd[4]:
                w = lastw.get(id(b))
                if w is not None:
                    deps.add(w)
                for r in readers.get(id(b), ()):
                    deps.add(r)
            deps.discard(i)
            for d in deps:
                succs[d].append(i)
            npred[i] = len(deps)
            for b in nd[3]:
                readers.setdefault(id(b), []).append(i)
            for b in nd[4]:
                lastw[id(b)] = i
                readers[id(b)] = []
        prio = [0.0] * nn
        for i in range(nn - 1, -1, -1):
            m = 0.0
            for s_ in succs[i]:
                if prio[s_] > m:
                    m = prio[s_]
            prio[i] = seg[i][6] + m
        rt = [0.0] * nn
        free_at = {e: 0.0 for e in self.ENG}
        ready = {e: [] for e in self.ENG}
        import bisect
        for i in range(nn):
            if npred[i] == 0:
                ready[seg[i][1]].append(i)
        order = []
        W = 24
        while len(order) < nn:
            best = None
            for e in self.ENG:
                rl = ready[e]
                if not rl:
                    continue
                fa = free_at[e]
                for i in rl[:W]:
                    st = rt[i] if rt[i] > fa else fa
                    key = (st, -prio[i], i)
                    if best is None or key < best[0]:
                        best = (key, e, i)
            key, e, i = best
            st = key[0]
            ready[e].remove(i)
            nd = seg[i]
            free_at[e] = st + nd[5]
            fin = st + nd[6]
            order.append(i)
            for s_ in succs[i]:
                if fin > rt[s_]:
                    rt[s_] = fin
                npred[s_] -= 1
                if npred[s_] == 0:
                    bisect.insort(ready[seg[s_][1]], s_)
        self.est_time = getattr(self, 'est_time', 0.0) + max(free_at.values())
        return order

    def _wait(self, eng, tok):
        if tok is None:
            return
        sk, val = tok
        if eng == 'pe' and sk == 'e_pe':
            return
        if self.seen[eng].get(sk, 0) >= val:
            return
        self.seen[eng][sk] = val
        self.prog[eng].append(('wait', self.sems[sk], val))

    def _deps(self, eng, reads, writes):
        for b in reads:
            self._wait(eng, b.w)
        for b in writes:
            self._wait(eng, b.w)
            for t in b.r:
                self._wait(eng, t)

    def _commit(self, tok, reads, writes):
        for b in reads:
            b.r.append(tok)
            if len(b.r) > 48:
                last = {}
                for t in b.r:
                    if last.get(t[0], 0) < t[1]:
                        last[t[0]] = t[1]
                b.r = list(last.items())
        for b in writes:
            b.w = tok
            b.r = []

    def _emit_op(self, nd):
        _, eng, fn, reads, writes = nd[:5]
        self._deps(eng, reads, writes)
        sk = 'e_' + eng
        self.cnt[sk] += 1
        tok = (sk, self.cnt[sk])
        self.prog[eng].append(('op', fn, self.sems[sk]))
        self._commit(tok, reads, writes)

    def _emit_dma(self, nd):
        _, q, (out, in_), reads, writes = nd[:5]
        self._deps(q, reads, writes)
        half = len(self.dma_sems) // 2
        if q == 'pool':
            base, span = half, half
        elif q == 'sp':
            base, span = 0, half - 4
        else:
            base, span = half - 4, 4
        sk = self.dma_sems[base + self.dma_rr[q]]
        self.dma_rr[q] = (self.dma_rr[q] + 1) % span
        if self.cnt[sk] > 0:
            self._wait(q, (sk, self.cnt[sk]))
        self.cnt[sk] += 16
        tok = (sk, self.cnt[sk])
        if q == 'pool':
            hist = self.__dict__.setdefault('pool_hist', [])
            if len(hist) >= 4:
                self._wait(q, hist[-4])
            hist.append(tok)
        self.prog[q].append(('dma', out, in_, self.sems[sk]))
        self._commit(tok, reads, writes)

    def _emit_barrier(self):
        for e in self.ENG:
            for sk, c in self.cnt.items():
                if c > 0:
                    self._wait(e, (sk, c))

    def emit(self):
        seg = []
        for nd in self.nodes + [['bar']]:
            if nd[0] == 'bar':
                for i in self._schedule_segment(seg):
                    if seg[i][0] == 'op':
                        self._emit_op(seg[i])
                    else:
                        self._emit_dma(seg[i])
                self._emit_barrier()
                seg = []
            else:
                seg.append(nd)
        nc = self.nc
        prog = self.prog

        def run(engname, e):
            for it in prog[engname]:
                if it[0] == 'wait':
                    e.wait_ge(it[1], it[2])
                elif it[0] == 'op':
                    it[1](e).then_inc(it[2], 1)
                else:
                    e.dma_start(out=it[1], in_=it[2]).then_inc(it[3], 16)

        with nc.Block() as block:
            @block.tensor
            def _(e):
                run('pe', e)

            @block.scalar
            def _(e):
                run('act', e)

            @block.vector
            def _(e):
                run('dve', e)

            @block.gpsimd
            def _(e):
                run('pool', e)

            @block.sync
            def _(e):
                run('sp', e)


class Arena:
    def __init__(self, t, nwords):
        self.t = t
        self.n = nwords
        self.top = 0

    def alloc(self, shape, dtype):
        nel = int(np.prod(shape))
        nbytes = nel * (2 if dtype == BF16 else 4)
        nw = (nbytes + 3) // 4
        nw = (nw + 7) // 8 * 8
        off = self.top
        self.top += nw
        assert self.top <= self.n, ('arena overflow', self.top, self.n)
        ap = self.t[:, off:off + nw]
        if dtype == BF16:
            ap = ap.bitcast(BF16)
        ap = ap[:, 0:nel]
        if len(shape) > 1:
            names = ['d%d' % i for i in range(len(shape))]
            pat = 'p (' + ' '.join(names) + ') -> p ' + ' '.join(names)
            ap = ap.rearrange(pat, **{n: s for n, s in zip(names, shape)})
        return ap

    def mark(self):
        return self.top

    def release(self, m):
        self.top = m


def bc(ap, shape):
    return ap.broadcast_to(shape)


def make_consts():
    i = np.arange(128)
    c = {}
    c['ident'] = np.eye(128, dtype=np.float32)
    c['ones'] = np.ones((128, 128), np.float32)
    c['U_p'] = (i[:, None] <= i[None, :]).astype(np.float32)
    c['SL_p'] = (i[:, None] > i[None, :]).astype(np.float32)
    c['MB_p'] = np.where(i[None, :] > i[:, None], 0.0, -30000.0).astype(np.float32)
    blk = i // 8
    same = blk[:, None] == blk[None, :]
    c['U_s'] = (same & (i[:, None] <= i[None, :])).astype(np.float32)
    c['SL_s'] = (same & (i[:, None] > i[None, :])).astype(np.float32)
    c['MB_s'] = np.where(same & (i[None, :] > i[:, None]), 0.0, -30000.0).astype(np.float32)
    c['M_cur'] = (i[:, None] <= i[None, :]).astype(np.float32)
    c['M_prev'] = (i[:, None] > i[None, :]).astype(np.float32)
    c['M_blk'] = (same & (i[:, None] <= i[None, :])).astype(np.float32)
    mc = np.zeros((128, 128), np.float32)
    mc[:, 0:8] = (i[:, None] >= (np.arange(8)[None, :] + 1)).astype(np.float32)
    c['M_cache'] = mc
    bm = np.zeros((128, 128), np.float32)
    bm[:, 0:16] = (blk[:, None] == np.arange(16)[None, :]).astype(np.float32)
    c['BM'] = bm
    names = ['ident', 'ones', 'U_p', 'SL_p', 'MB_p', 'U_s', 'SL_s', 'MB_s', 'M_cur', 'M_prev', 'M_blk', 'M_cache', 'BM']
    arr = np.stack([c[n] for n in names], axis=1)
    cm = np.broadcast_to((np.arange(16)[:, None] == blk[None, :]).astype(np.float32)[None], (128, 16, 128))
    return names, np.ascontiguousarray(arr), np.ascontiguousarray(cm)


CONST_NAMES = ['ident', 'ones', 'U_p', 'SL_p', 'MB_p', 'U_s', 'SL_s', 'MB_s', 'M_cur', 'M_prev', 'M_blk', 'M_cache', 'BM']


def build_nc(stop_after=None, skip_ffn=False):
    nc = bass.Bass("TRN2", target_bir_lowering=False)

    def din(name, shape):
        return nc.dram_tensor(name, list(shape), F32, kind="ExternalInput").ap()

    def dout(name, shape):
        return nc.dram_tensor(name, list(shape), F32, kind="ExternalOutput").ap()

    x_p = din('x_p', [2048, D])
    x_s = din('x_s', [128, D])
    cache_k = din('cache_k', [16, 128, 128])
    cache_v = din('cache_v', [16, 128, 128])
    st_conv = din('st_conv', [48, 1536])
    st_delta = din('st_delta', [16, 4, 128, 128])
    consts_d = din('consts', [128, len(CONST_NAMES), 128])
    colmask_d = din('colmask', [128, 16, 128])
    gpost_d = din('gpost', [128, 3, D])
    gpreT_d = din('gpreT', [128, 3, 8])
    smallp_d = din('smallp', [128, 32])
    convw_d = din('convw', [128, 12, 4])
    w_up = [din('w_up1', [D, 2 * DFF]), din('w_up2', [D, 2 * DFF])]
    w_down = [din('w_down1', [DFF, D]), din('w_down2', [DFF, D])]
    w_fm = din('w_fm', [D, 4736])
    w_tm = din('w_tm', [D, 320])
    w_ba = din('w_ba', [512, D])
    w_bb = din('w_bb', [512, D])
    w_o = din('w_o', [D, D])

    y_p = dout('y_p', [2048, D])
    y_s = dout('y_s', [128, D])
    nk_p = dout('nk_p', [128, 128])
    nv_p = dout('nv_p', [128, 128])
    ncv_p = dout('ncv_p', [3, 1536])
    nd_p = dout('nd_p', [4, 128, 128])
    nk_s = dout('nk_s', [16, 128, 128])
    nv_s = dout('nv_s', [16, 128, 128])
    ncv_s = dout('ncv_s', [16, 3, 1536])
    nd_s = dout('nd_s', [16, 4, 128, 128])

    k = KB(nc)
    NW = 53000
    arena_t = nc.alloc_sbuf_tensor("arena", [128, NW], F32)
    A = Arena(arena_t, NW)
    ps = nc.alloc_psum_tensor("ps", [128, 8, 512], F32)
    PB = [Buf('ps%d' % i, excl=True) for i in range(8)]

    def psb(b, n=512):
        return ps[:, b, 0:n]

    def psb16(b):
        return ps[:, b, :].bitcast(BF16)

    NC_ = len(CONST_NAMES)
    cf = A.alloc([NC_, 128], F32)
    cfB = Buf('cf')
    CI = {n: i for i, n in enumerate(CONST_NAMES)}

    def C(n):
        return cf[:, CI[n], :]
    cb = A.alloc([NC_, 128], BF16)
    cbB = Buf('cb')

    def Cb(n):
        return cb[:, CI[n], :]
    gpost = A.alloc([3, D], F32)
    gpostB = Buf()
    gpreT = A.alloc([3, 8], F32)
    gpreTB = Buf()
    smallp = A.alloc([32], F32)
    smallpB = Buf()
    negA = A.alloc([4], F32)
    esink = A.alloc([8], F32)
    derivB = Buf()
    convw = A.alloc([12, 4], F32)
    convwB = Buf()
    NTMAX = 6
    X = A.alloc([NTMAX, D], F32)
    XB = [Buf('X%d' % i) for i in range(NTMAX)]
    hT = A.alloc([8, NTMAX * 128], BF16)
    hTB = [Buf('hT%d' % i) for i in range(NTMAX)]
    S = A.alloc([4, 128], F32)
    Sbf = A.alloc([4, 128], BF16)
    SB = Buf('S')
    SbfB = Buf('Sbf')
    convhist = A.alloc([12, 3], F32)
    convhistB = Buf('convhist')
    kT = A.alloc([128 + NTMAX * 128], BF16)
    kTB = [Buf('kT%d' % i) for i in range(NTMAX + 1)]
    KV = A.alloc([NTMAX + 1, 256], BF16)
    KVB = [Buf('KV%d' % i) for i in range(NTMAX + 1)]
    WS = [A.alloc([8, 256], BF16) for _ in range(6)]
    WSB = [Buf('ws%d' % i) for i in range(6)]
    persist_mark = A.mark()

    k.dma('sp', cf, consts_d[:, :, :], writes=[cfB])
    k.dma('sp', gpost, gpost_d[:, :, :], writes=[gpostB])
    k.dma('sp', gpreT, gpreT_d[:, :, :], writes=[gpreTB])
    k.dma('sp', smallp, smallp_d[:, :], writes=[smallpB])
    k.dma('sp', convw, convw_d[:, :, :], writes=[convwB])
    k.op('dve', lambda e: e.tensor_copy(out=cb, in_=cf), reads=[cfB], writes=[cbB])
    k.op('act', lambda e: e.activation(out=negA, in_=smallp[:, 0:4], func=AF.Exp), reads=[smallpB], writes=[derivB])
    k.op('act', lambda e: e.activation(out=esink, in_=smallp[:, 8:16], func=AF.Exp), reads=[smallpB], writes=[derivB])
    k.op('dve', lambda e: e.tensor_scalar(out=negA, in0=negA, scalar1=-1.0, scalar2=None, op0=ALU.mult),
         reads=[derivB], writes=[derivB])
    k.op('pool', lambda e: e.memset(S, 0.0), writes=[SB])
    k.op('pool', lambda e: e.memset(Sbf, 0.0), writes=[SbfB])
    k.op('pool', lambda e: e.memset(convhist, 0.0), writes=[convhistB])

    def tok_groups(T):
        if T == 640:
            return [(0, 320), (320, 320)]
        g = []
        o = 0
        while o < T:
            n = min(512, T - o)
            g.append((o, n))
            o += n
        return g

    def prenorm(nt, which, ctx=None):
        if ctx is not None and 'pn' in ctx:
            sq, sqB, ssq, rstd, stB, hb, hbB = ctx['pn']
        else:
            sq = [A.alloc([D], F32) for _ in range(2)]
            sqB = [Buf(), Buf()]
            ssq = A.alloc([NTMAX], F32)
            rstd = A.alloc([NTMAX], F32)
            stB = [Buf() for _ in range(NTMAX)]
            hb = A.alloc([2, D], BF16)
            hbB = [Buf(), Buf()]
            if ctx is not None:
                ctx['pn'] = (sq, sqB, ssq, rstd, stB, hb, hbB)
        for t in range(nt):
            s = t % 2
            k.op('act', lambda e, t=t, s=s: e.activation(out=sq[s], in_=X[:, t, :], func=AF.Square, accum_out=ssq[:, t:t + 1]),
                 reads=[XB[t]], writes=[sqB[s], stB[t]], n=1024)
            k.op('act', lambda e, t=t: e.activation(out=rstd[:, t:t + 1], in_=ssq[:, t:t + 1], func=AF.Sqrt, bias=EPS,
                                                    scale=1.0 / D), reads=[stB[t]], writes=[stB[t]], n=1)
            k.op('dve', lambda e, t=t: e.reciprocal(out=rstd[:, t:t + 1], in_=rstd[:, t:t + 1]), reads=[stB[t]], writes=[stB[t]], n=1)
            k.op('dve', lambda e, t=t, s=s: e.tensor_scalar(out=hb[:, s, :], in0=X[:, t, :], scalar1=rstd[:, t:t + 1],
                                                            scalar2=None, op0=ALU.mult),
                 reads=[XB[t], stB[t]], writes=[hbB[s]], n=1024)
            bank = 6 + s
            for c in range(8):
                k.op('pe', lambda e, c=c, s=s, bank=bank: e.transpose(
                    out=psb16(bank)[:, c * 128:(c + 1) * 128], in_=hb[:, s, c * 128:(c + 1) * 128], identity=Cb('ident')),
                    reads=[hbB[s], cbB], writes=[PB[bank]], n=128)
            k.op('dve', lambda e, t=t, bank=bank: e.tensor_tensor(
                out=hT[:, :, t * 128:(t + 1) * 128], in0=psb16(bank).rearrange("p (c n) -> p c n", n=128),
                in1=bc(gpreT[:, which, :].unsqueeze(2), [128, 8, 128]), op=ALU.mult),
                reads=[PB[bank], gpreTB], writes=[hTB[t]], n=1024)

    def make_post(ctx=None):
        if ctx is not None and 'post' in ctx:
            junk, ss2, r, tmp, jB, sB, tB = ctx['post']
        else:
            junk = A.alloc([512], F32)
            ss2 = A.alloc([2], F32)
            r = A.alloc([1], F32)
            tmp = A.alloc([D], F32)
            jB, sB, tB = Buf(), Buf(), Buf()
            if ctx is not None:
                ctx['post'] = (junk, ss2, r, tmp, jB, sB, tB)

        def post(t, banks, which, half):
            for hh in range(2):
                k.op('act', lambda e, hh=hh: e.activation(out=junk, in_=psb(banks[hh]), func=AF.Square,
                                                          accum_out=ss2[:, hh:hh + 1]),
                     reads=[PB[banks[hh]]], writes=[jB, sB], n=512)
            k.op('dve', lambda e: e.tensor_tensor(out=r, in0=ss2[:, 0:1], in1=ss2[:, 1:2], op=ALU.add),
                 reads=[sB], writes=[sB])
            sc = 4.0 if half else 1.0
            k.op('act', lambda e: e.activation(out=r, in_=r, func=AF.Sqrt, bias=EPS * sc, scale=sc / D),
                 reads=[sB], writes=[sB])
            k.op('dve', lambda e: e.reciprocal(out=r, in_=r), reads=[sB], writes=[sB])
            for hh in range(2):
                k.op('dve', lambda e, hh=hh: e.scalar_tensor_tensor(
                    out=tmp[:, hh * 512:(hh + 1) * 512], in0=psb(banks[hh]), scalar=r[:, 0:1],
                    in1=gpost[:, which, hh * 512:(hh + 1) * 512], op0=ALU.mult, op1=ALU.mult),
                    reads=[PB[banks[hh]], sB, gpostB], writes=[tB])
            k.op('pool', lambda e: e.tensor_tensor(out=X[:, t, :], in0=X[:, t, :], in1=tmp, op=ALU.add),
                 reads=[tB, XB[t]], writes=[XB[t]], n=1024)
        return post

    FCTX = {}

    def ffn_wload(which, fb):
        s = fb % 3
        wup = w_up[which]
        k.dma('pool', WS[2 * s], wup[:, fb * 256:(fb + 1) * 256].rearrange("(c p) n -> p c n", p=128), writes=[WSB[2 * s]])
        k.dma('pool', WS[2 * s + 1], wup[:, DFF + fb * 256:DFF + (fb + 1) * 256].rearrange("(c p) n -> p c n", p=128),
              writes=[WSB[2 * s + 1]])

    def ffn(nt, which, pre=0):
        T = nt * 128
        m = A.mark()
        assert m == persist_mark
        prenorm(nt, 0 if which == 0 else 2, FCTX)
        post = make_post(FCTX)
        WDG = [(0, 6), (6, 12), (12, 17), (17, 22)]
        wdOf = [0] * 6 + [1] * 6 + [2] * 5 + [3] * 5
        if 'main' in FCTX:
            actT, actB, wd, wdB, sg, sgB, top_ = FCTX['main']
            A.top = top_
        else:
            actT = A.alloc([22, NTMAX * 128], BF16)
            actB = [Buf('act%d' % j) for j in range(22)]
            wd = A.alloc([22, D], BF16)
            wdB = [Buf('wd%d' % i) for i in range(4)]
        NWB = 3
        wg = [WS[2 * i] for i in range(NWB)]
        wu = [WS[2 * i + 1] for i in range(NWB)]
        wgB = [WSB[2 * i] for i in range(NWB)]
        wuB = [WSB[2 * i + 1] for i in range(NWB)]
        if 'main' not in FCTX:
            sg = [A.alloc([512], BF16) for _ in range(2)]
            sgB = [Buf(), Buf()]
            FCTX['main'] = (actT, actB, wd, wdB, sg, sgB, A.top)
        groups = tok_groups(T)
        wup = w_up[which]
        for gi_, (j0, j1) in enumerate(WDG):
            k.dma('pool', wd[:, j0:j1, :], w_down[which][j0 * 128:j1 * 128, :].rearrange("(j p) d -> p j d", p=128),
                  writes=[wdB[gi_]], nbytes=(j1 - j0) * 128 * 4096)
        gi = 0
        for fb in range(11):
            s = fb % NWB
            if fb >= pre:
                ffn_wload(which, fb)
            for jj in range(2):
                j = 2 * fb + jj
                for (o, n) in groups:
                    bg = (gi % 3) * 2
                    bu = bg + 1
                    ss_ = gi % 2
                    gi += 1
                    for c in range(8):
                        k.op('pe', lambda e, c=c, s=s, jj=jj, o=o, n=n, bg=bg: e.matmul(
                            psb(bg, n), lhsT=wg[s][:, c, jj * 128:(jj + 1) * 128], rhs=hT[:, c, o:o + n],
                            start=(c == 0), stop=(c == 7)),
                            reads=[wgB[s]] + hTB[o // 128:(o + n + 127) // 128], writes=[PB[bg]], n=n)
                    for c in range(8):
                        k.op('pe', lambda e, c=c, s=s, jj=jj, o=o, n=n, bu=bu: e.matmul(
                            psb(bu, n), lhsT=wu[s][:, c, jj * 128:(jj + 1) * 128], rhs=hT[:, c, o:o + n],
                            start=(c == 0), stop=(c == 7)),
                            reads=[wuB[s]] + hTB[o // 128:(o + n + 127) // 128], writes=[PB[bu]], n=n)
                    k.op('act', lambda e, bg=bg, n=n, ss_=ss_: e.activation(out=sg[ss_][:, 0:n], in_=psb(bg, n), func=AF.Silu),
                         reads=[PB[bg]], writes=[sgB[ss_]], n=n)
                    k.op('dve', lambda e, bu=bu, n=n, o=o, j=j, ss_=ss_: e.tensor_tensor(
                        out=actT[:, j, o:o + n], in0=psb(bu, n), in1=sg[ss_][:, 0:n], op=ALU.mult),
                        reads=[PB[bu], sgB[ss_]], writes=[actB[j]], n=n)
        for t in range(nt):
            banks = [6, 7] if t % 2 == 0 else [4, 5]
            for hh in range(2):
                for j in range(22):
                    k.op('pe', lambda e, t=t, hh=hh, j=j, banks=banks: e.matmul(
                        psb(banks[hh]), lhsT=actT[:, j, t * 128:(t + 1) * 128], rhs=wd[:, j, hh * 512:(hh + 1) * 512],
                        start=(j == 0), stop=(j == 21)),
                        reads=[actB[j], wdB[wdOf[j]]], writes=[PB[banks[hh]]])
            post(t, banks, 0 if which == 0 else 2, True)
        A.release(m)


    FM_Q, FM_K, FM_QKV, FM_ZB, FM_GA, FM_GB = 0, 512, 640, 2176, 2688, 3712
    ISQ = 128 ** -0.5

    M1_BLK_ORDER = [2, 3, 4, 5, 6, 7, 8, 0, 1, 9, 10]

    def m1_wload(pos):
        blk = M1_BLK_ORDER[pos]
        s = pos % 3
        wcols = min(256, 2688 - blk * 256)
        k.dma('pool', WS[s][:, :, 0:wcols], w_fm[:, blk * 256:blk * 256 + wcols].rearrange("(c p) n -> p c n", p=128),
              writes=[WSB[s]])

    def m3_wload(blk):
        s = blk % 2
        k.dma('pool', WS[s], w_fm[:, FM_GA + blk * 256:FM_GA + (blk + 1) * 256].rearrange("(c p) n -> p c n", p=128),
              writes=[WSB[s]])
        k.dma('pool', WS[2 + s], w_fm[:, FM_GB + blk * 256:FM_GB + (blk + 1) * 256].rearrange("(c p) n -> p c n", p=128),
              writes=[WSB[2 + s]])
        k.dma('pool', WS[4][:, 4 * s:4 * s + 4, :], w_ba[:, blk * 256:(blk + 1) * 256].rearrange("(c p) n -> p c n", p=128),
              writes=[WSB[4]], nbytes=1 << 19)
        k.dma('pool', WS[5][:, 4 * s:4 * s + 4, :], w_bb[:, blk * 256:(blk + 1) * 256].rearrange("(c p) n -> p c n", p=128),
              writes=[WSB[5]], nbytes=1 << 19)

    def mixer(nt, npt, has_s, t0, pre1=0, pre3=0, next_ffn_pre=0):
        T = nt * 128
        Tp = npt * 128
        groups = tok_groups(T)
        m0 = A.mark()
        qT = A.alloc([4, NTMAX * 128], BF16)
        qTB = [Buf() for _ in range(NTMAX)]
        KQV = A.alloc([NTMAX, 4, 3, 128], BF16)
        KQVB = [Buf() for _ in range(NTMAX)]
        zsT = A.alloc([4, NTMAX * 128], BF16)
        zsB = [Buf() for _ in range(NTMAX)]
        oaT = A.alloc([4, NTMAX * 128], BF16)
        oaB = [Buf() for _ in range(NTMAX)]
        obT = A.alloc([4, NTMAX * 128], BF16)
        obB = [Buf() for _ in range(NTMAX)]
        gbraw = A.alloc([NTMAX, 8], F32)
        gcol = A.alloc([NTMAX, 4], F32)
        betac = A.alloc([NTMAX, 4], F32)
        gbB = Buf()
        gB = Buf()
        outB = Buf()
        outB2 = [Buf(), Buf()]

        def tb(bl, o, n):
            return bl[o // 128:(o + n + 127) // 128]

        m1 = A.mark()
        kvout = A.alloc([2, 256], F32)
        kvoB = [Buf(), Buf()]
        cvout = A.alloc([2, 1536], F32)
        cvoB = [Buf(), Buf()]
        prenorm(nt, 1)
        wtm = A.alloc([8, 320], BF16)
        wtmB = Buf()
        k.dma('pool', wtm, w_tm.rearrange("(c p) n -> p c n", p=128), writes=[wtmB])
        NWB = 3
        wblk = [WS[i] for i in range(NWB)]
        wblkB = [WSB[i] for i in range(NWB)]
        st = [A.alloc([3 + NTMAX * 128], F32) for _ in range(2)]
        stB = [Buf(), Buf()]
        acc = [A.alloc([NTMAX * 128], F32) for _ in range(2)]
        accB = [Buf(), Buf()]
        sil2 = [A.alloc([NTMAX * 128], F32) for _ in range(2)]
        silB2 = [Buf(), Buf()]
        sqb2 = [A.alloc([NTMAX * 128], BF16) for _ in range(2)]
        sqbB2 = [Buf(), Buf()]
        rn2 = [A.alloc([NTMAX * 128], F32) for _ in range(2)]
        rnB2 = [Buf(), Buf()]
        if has_s:
            ss = A.alloc([16, 11], F32)
            ssB = Buf()
            scraw = A.alloc([1536], F32)
            scrawB = Buf()
            scT = A.alloc([12, 48], F32)
            scTB = Buf()
            k.dma('sp', scraw[0:48, :], st_conv[:, :], writes=[scrawB])
            for half in (range(2) if 'T' not in DBG else []):
                for c6 in range(6):
                    ch = half * 6 + c6
                    k.op('pe', lambda e, ch=ch, c6=c6, half=half: e.transpose(
                        out=ps[:, half, c6 * 48:(c6 + 1) * 48], in_=scraw[0:48, ch * 128:(ch + 1) * 128],
                        identity=C('ident')[0:48, 0:48]), reads=[scrawB, cfB], writes=[PB[half]])
                k.op('dve', lambda e, half=half: e.tensor_copy(
                    out=scT[:, half * 6:(half + 1) * 6, :], in_=ps[:, half, 0:288].rearrange("p (c n) -> p c n", n=48)),
                    reads=[PB[half]], writes=[scTB])

        special = {}
        if t0 + npt == 16:
            special[npt - 1] = 0
        if has_s:
            special[npt] = 1

        for t in (range(nt) if 'M' not in DBG else []):
            bank = 6 + (t % 2)
            for c in range(8):
                k.op('pe', lambda e, t=t, c=c, bank=bank: e.matmul(
                    psb(bank, 256), lhsT=hT[:, c, t * 128:(t + 1) * 128], rhs=wtm[:, c, 0:256], start=(c == 0), stop=(c == 7)),
                    reads=[hTB[t], wtmB], writes=[PB[bank]])
            for c in (range(8) if 'm' not in DBG else []):
                k.op('pe', lambda e, t=t, c=c, bank=bank: e.matmul(
                    ps[:, bank, 256:320], lhsT=hT[:, c, t * 128:(t + 1) * 128], rhs=wtm[:, c, 256:320], start=(c == 0), stop=(c == 7)),
                    reads=[hTB[t], wtmB], writes=[PB[bank]])
            if 'e' in DBG:
                continue
            k.op('act', lambda e, t=t, bank=bank: e.copy(out=KV[:, 1 + t, :], in_=psb(bank, 256)),
                 reads=[PB[bank]], writes=[KVB[1 + t]])
            if 'd' in DBG:
                continue
            if 'm' not in DBG:
                k.op('act', lambda e, t=t, bank=bank: e.copy(out=gbraw[:, t, :], in_=ps[:, bank, (0 if 'z' in DBG else 256):(8 if 'z' in DBG else 264)]),
                     reads=[PB[bank]], writes=[gbB])
            if t in special:
                si = special[t]
                k.op('dve', lambda e, si=si, bank=bank: e.tensor_copy(out=kvout[:, si, :], in_=psb(bank, 256)),
                     reads=[PB[bank]], writes=[kvoB[si]])
        if 'G' not in DBG:
            k.op('dve', lambda e: e.tensor_tensor(out=gcol[:, 0:nt, :], in0=gbraw[:, 0:nt, 0:4],
                                                  in1=bc(smallp[:, 4:8].unsqueeze(1), [128, nt, 4]), op=ALU.add),
                 reads=[gbB, smallpB], writes=[gB])
            k.op('act', lambda e: e.activation(out=gcol[:, 0:nt, :], in_=gcol[:, 0:nt, :], func=AF.Exp), reads=[gB], writes=[gB])
            k.op('act', lambda e: e.activation(out=gcol[:, 0:nt, :], in_=gcol[:, 0:nt, :], func=AF.Ln, bias=1.0),
                 reads=[gB], writes=[gB])
            k.op('dve', lambda e: e.tensor_tensor(out=gcol[:, 0:nt, :], in0=gcol[:, 0:nt, :],
                                                  in1=bc(negA.unsqueeze(1), [128, nt, 4]), op=ALU.mult),
                 reads=[gB, derivB], writes=[gB])
            k.op('act', lambda e: e.activation(out=betac[:, 0:nt, :], in_=gbraw[:, 0:nt, 4:8], func=AF.Sigmoid),
                 reads=[gbB], writes=[gB])

        gi = 0
        ci_order = [ci_ for b_ in M1_BLK_ORDER for ci_ in (2 * b_, 2 * b_ + 1) if ci_ < 21]
        started = set()
        for ci in (ci_order if 'F' not in DBG else []):
            blk = ci // 2
            pos = M1_BLK_ORDER.index(blk)
            s = pos % NWB
            if blk not in started:
                started.add(blk)
                if pos >= pre1:
                    m1_wload(pos)
            wv = wblk[s][:, :, (ci % 2) * 128:(ci % 2) * 128 + 128]
            ch = ci - 5
            sidx = ch % 2
            for (o, n) in groups:
                bank = gi % 6
                gi += 1
                for c in range(8):
                    k.op('pe', lambda e, c=c, wv=wv, o=o, n=n, bank=bank: e.matmul(
                        psb(bank, n), lhsT=wv[:, c, :], rhs=hT[:, c, o:o + n], start=(c == 0), stop=(c == 7)),
                        reads=[wblkB[s]] + tb(hTB, o, n), writes=[PB[bank]], n=n)
                if ci < 4:
                    k.op('act', lambda e, ci=ci, o=o, n=n, bank=bank: e.activation(
                        out=qT[:, ci, o:o + n], in_=psb(bank, n), func=AF.Copy, scale=0.125),
                        reads=[PB[bank]], writes=tb(qTB, o, n))
                elif ci == 4:
                    k.op('act', lambda e, o=o, n=n, bank=bank: e.copy(out=kT[:, 128 + o:128 + o + n], in_=psb(bank, n)),
                         reads=[PB[bank]], writes=tb(kTB[1:], o, n))
                elif ci < 17:
                    k.op('act', lambda e, o=o, n=n, bank=bank, sidx=sidx: e.copy(
                        out=st[sidx][:, 3 + o:3 + o + n], in_=psb(bank, n)), reads=[PB[bank]], writes=[stB[sidx]])
                else:
                    k.op('act', lambda e, ci=ci, o=o, n=n, bank=bank: e.activation(
                        out=zsT[:, ci - 17, o:o + n], in_=psb(bank, n), func=AF.Silu),
                        reads=[PB[bank]], writes=tb(zsB, o, n))
            if 5 <= ci < 17 and 'C' not in DBG:
                for t, si in special.items():
                    bank = 6 + (gi % 2)
                    gi += 1
                    for c in range(8):
                        k.op('pe', lambda e, c=c, t=t, wv=wv, bank=bank: e.matmul(
                            psb(bank, 128), lhsT=hT[:, c, t * 128:(t + 1) * 128], rhs=wv[:, c, :], start=(c == 0), stop=(c == 7)),
                            reads=[hTB[t], wblkB[s]], writes=[PB[bank]])
                    k.op('dve', lambda e, si=si, ch=ch, bank=bank: e.tensor_copy(
                        out=cvout[:, si, ch * 128:(ch + 1) * 128], in_=psb(bank, 128)), reads=[PB[bank]], writes=[cvoB[si]])
                sb_, ab_ = st[sidx], acc[sidx]
                k.op('pool', lambda e, sb_=sb_, ch=ch: e.tensor_copy(out=sb_[:, 0:3], in_=convhist[:, ch, :]),
                     reads=[convhistB], writes=[stB[sidx]])
                k.op('dve', lambda e, sb_=sb_, ab_=ab_, ch=ch: e.tensor_scalar(
                    out=ab_[:, 0:Tp], in0=sb_[:, 3:3 + Tp], scalar1=convw[:, ch, 3:4], scalar2=None, op0=ALU.mult),
                    reads=[stB[sidx], convwB], writes=[accB[sidx]])
                for j in range(3):
                    k.op('dve', lambda e, sb_=sb_, ab_=ab_, ch=ch, j=j: e.scalar_tensor_tensor(
                        out=ab_[:, 0:Tp], in0=sb_[:, j:j + Tp], scalar=convw[:, ch, j:j + 1], in1=ab_[:, 0:Tp],
                        op0=ALU.mult, op1=ALU.add), reads=[stB[sidx], convwB, accB[sidx]], writes=[accB[sidx]])
                if has_s:
                    k.op('pool', lambda e, sb_=sb_: e.tensor_copy(
                        out=ss[:, :, 3:11], in_=sb_[:, 3 + Tp:3 + Tp + 128].rearrange("p (b t) -> p b t", t=8)),
                        reads=[stB[sidx]], writes=[ssB])
                    k.op('pool', lambda e, ch=ch: e.tensor_copy(
                        out=ss[:, :, 0:3], in_=scT[:, ch, :].rearrange("p (b t) -> p b t", t=3)),
                        reads=[scTB], writes=[ssB])
                    av = ab_[:, Tp:Tp + 128].rearrange("p (b t) -> p b t", t=8)
                    k.op('dve', lambda e, av=av, ch=ch: e.tensor_scalar(
                        out=av, in0=ss[:, :, 3:11], scalar1=convw[:, ch, 3:4], scalar2=None, op0=ALU.mult),
                        reads=[ssB, convwB], writes=[accB[sidx]])
                    for j in range(3):
                        k.op('dve', lambda e, av=av, ch=ch, j=j: e.scalar_tensor_tensor(
                            out=av, in0=ss[:, :, j:j + 8], scalar=convw[:, ch, j:j + 1], in1=av,
                            op0=ALU.mult, op1=ALU.add), reads=[ssB, convwB, accB[sidx]], writes=[accB[sidx]])
                k.op('pool', lambda e, sb_=sb_, ch=ch: e.tensor_copy(out=convhist[:, ch, :], in_=sb_[:, Tp:Tp + 3]),
                     reads=[stB[sidx]], writes=[convhistB])
                typ, h = ch // 4, ch % 4
                sil, silB, sqb, sqbB, rn, rnB = sil2[sidx], silB2[sidx], sqb2[sidx], sqbB2[sidx], rn2[sidx], rnB2[sidx]
                if typ == 2:
                    k.op('act', lambda e, ab_=ab_, h=h, sil=sil, sqb=sqb, rn=rn: e.activation(
                        out=KQV[:, 0:nt, h, 2, :], in_=ab_[:, 0:T].rearrange("p (t n) -> p t n", n=128), func=AF.Silu),
                        reads=[accB[sidx]], writes=KQVB[0:nt])
                else:
                    k.op('act', lambda e, ab_=ab_, sil=sil, sqb=sqb, rn=rn: e.activation(out=sil[:, 0:T], in_=ab_[:, 0:T], func=AF.Silu),
                         reads=[accB[sidx]], writes=[silB])
                    k.op('pool', lambda e, sil=sil, sqb=sqb, rn=rn: e.tensor_tensor(out=sqb[:, 0:T], in0=sil[:, 0:T], in1=sil[:, 0:T], op=ALU.mult),
                         reads=[silB], writes=[sqbB])
                    for (o, n) in groups:
                        bank = gi % 6
                        gi += 1
                        k.op('pe', lambda e, o=o, n=n, bank=bank, sil=sil, sqb=sqb, rn=rn: e.matmul(
                            psb(bank, n), lhsT=Cb('ones'), rhs=sqb[:, o:o + n], start=True, stop=True),
                            reads=[sqbB, cbB], writes=[PB[bank]])
                        k.op('act', lambda e, o=o, n=n, bank=bank, typ=typ, sil=sil, sqb=sqb, rn=rn: e.activation(
                            out=rn[:, o:o + n], in_=psb(bank, n), func=AF.Ln, bias=EPS * (128.0 if typ == 0 else 1.0),
                            scale=(128.0 if typ == 0 else 1.0)), reads=[PB[bank]], writes=[rnB], n=n)
                    k.op('act', lambda e, sil=sil, sqb=sqb, rn=rn: e.activation(out=rn[:, 0:T], in_=rn[:, 0:T], func=AF.Exp, scale=-0.5),
                         reads=[rnB], writes=[rnB], n=T)
                    slot = 1 if typ == 0 else 0
                    k.op('dve', lambda e, h=h, slot=slot, sil=sil, sqb=sqb, rn=rn: e.tensor_tensor(
                        out=KQV[:, 0:nt, h, slot, :], in0=sil[:, 0:T].rearrange("p (t n) -> p t n", n=128),
                        in1=rn[:, 0:T].rearrange("p (t n) -> p t n", n=128), op=ALU.mult),
                        reads=[silB, rnB], writes=KQVB[0:nt], n=T)
        for t, si in (special.items() if 'O' not in DBG else []):
            if si == 0:
                k.dma('sp', nk_p[:, :], kvout[:, 0, 0:128], reads=[kvoB[0]], writes=[outB])
                k.dma('sp', nv_p[:, :], kvout[:, 0, 128:256], reads=[kvoB[0]], writes=[outB])
                k.dma('sp', ncv_p[:, :], cvout[125:128, 0, :], reads=[cvoB[0]], writes=[outB])
            else:
                k.dma('sp', nk_s[:, 0:120, :], cache_k[:, 8:128, :], writes=[outB])
                k.dma('sp', nv_s[:, 0:120, :], cache_v[:, 8:128, :], writes=[outB])
                for b in range(16):
                    q_ = 'sp' if b % 2 == 0 else 'act'
                    k.dma(q_, nk_s[b, 120:128, :], kvout[8 * b:8 * b + 8, 1, 0:128], reads=[kvoB[1]], writes=[outB2[b % 2]],
                          nbytes=4096)
                    k.dma(q_, nv_s[b, 120:128, :], kvout[8 * b:8 * b + 8, 1, 128:256], reads=[kvoB[1]], writes=[outB2[b % 2]],
                          nbytes=4096)
                    k.dma(q_, ncv_s[b, :, :], cvout[8 * b + 5:8 * b + 8, 1, :], reads=[cvoB[1]], writes=[outB2[b % 2]],
                          nbytes=18432)
        k.barrier()
        A.release(m1)
        if stop_after == 'm1':
            A.release(m0)
            return

        m2 = A.mark()
        Pc = A.alloc([2, 512], BF16)
        Pp = A.alloc([2, 512], BF16)
        PcB, PpB = [Buf(), Buf()], [Buf(), Buf()]
        rd = A.alloc([2, 512], F32)
        rdB = [Buf(), Buf()]
        if has_s:
            Sst = A.alloc([16, 128], F32)
            ckcv = Sst.rearrange("p b n -> p (b n)").bitcast(BF16).rearrange("p (b x) -> p b x", x=256)
            ck = ckcv[:, :, 0:128]
            cv = ckcv[:, :, 128:256]
            kcT = A.alloc([16, 128], BF16)
            kcTB = Buf()
            ckB2, cvB2 = [Buf(), Buf()], [Buf(), Buf()]
            for hf in range(2):
                k.dma('pool', ck[:, hf * 8:(hf + 1) * 8, :], cache_k[hf * 8:(hf + 1) * 8].rearrange("b s d -> s b d"),
                      writes=[ckB2[hf]], nbytes=1 << 19)
                k.dma('pool', cv[:, hf * 8:(hf + 1) * 8, :], cache_v[hf * 8:(hf + 1) * 8].rearrange("b s d -> s b d"),
                      writes=[cvB2[hf]], nbytes=1 << 19)
            Ssb = kcT
            Zk = A.alloc([16, 128], BF16)
            kdm = A.alloc([16, 128], BF16)
            SstB, SsbB, ZkB, kdmB = Buf(), kcTB, Buf(), Buf()
            k.op('pool', lambda e: e.memset(Zk, 0.0), writes=[ZkB], n=2048)
        Ug = A.alloc([4, 128], F32)
        E2m = A.alloc([4, 128], F32)
        D2s = A.alloc([4, 128], F32)
        EG = A.alloc([4, 128], F32)
        D2b = A.alloc([4, 128], F32)
        D2i = A.alloc([4, 128], F32)
        kdsc = A.alloc([4], F32)
        DT_INV = F32 if 'B' not in DBG else BF16
        Nn = A.alloc([4, 128], DT_INV)
        Nt = A.alloc([4, 128], DT_INV)
        Pw = [A.alloc([4, 128], DT_INV) for _ in range(2)]
        Ptw = [A.alloc([4, 128], DT_INV) for _ in range(2)]
        Yw = [A.alloc([4, 128], DT_INV) for _ in range(2)]
        Xs_dt = DT_INV
        QKd = A.alloc([4, 128], BF16)
        qd = A.alloc([4, 128], BF16)
        kdn = A.alloc([4, 128], BF16)
        kdec = A.alloc([4, 128], BF16)
        Xs = A.alloc([4, 128], Xs_dt)
        vnew = A.alloc([4, 128], BF16)
        tmpS = A.alloc([4, 128], F32)
        osq = A.alloc([4, 128], BF16)
        orn = A.alloc([4, 128], F32)
        otmp = A.alloc([4, 128], F32)
        (UgB, E2mB, D2sB, EGB, D2bB, D2iB, kdscB, NnB, NtB, QKdB, qdB, kdnB, kdecB, XsB, vnewB, tmpSB, osqB, ornB,
         otmpB) = [Buf() for _ in range(19)]
        MIXP = 1 if 'X' in DBG else 0
        if MIXP:
            Pb = [A.alloc([4, 128], BF16) for _ in range(2)]
            Ptb = [A.alloc([4, 128], BF16) for _ in range(2)]
            Yb = A.alloc([4, 128], BF16)
        else:
            Pb = Ptb = [None, None]
            Yb = None
        PbB, PtbB, YbB = [Buf(), Buf()], [Buf(), Buf()], Buf()
        NPAR = 1 if has_s else 2
        DB = [(EG, EGB, QKd, QKdB, qd, qdB, kdn, kdnB, kdec, kdecB, Yw, [Buf(), Buf()])]
        if NPAR == 2:
            DB.append((A.alloc([4, 128], F32), Buf(), A.alloc([4, 128], BF16), Buf(), A.alloc([4, 128], BF16), Buf(),
                       A.alloc([4, 128], BF16), Buf(), A.alloc([4, 128], BF16), Buf(),
                       [A.alloc([4, 128], DT_INV) for _ in range(2)], [Buf(), Buf()]))
        PwB = [Buf(), Buf()]
        PtwB = [Buf(), Buf()]
        YwB = [Buf(), Buf()]

        def bank4(b):
            return ps[:, b, :].rearrange("p (c n) -> p c n", n=128)

        A0, A1 = 0, 1

        def attn_group_scores(t, g, bank, kcol, Pbuf, PB_, mask):
            for c in range(4):
                k.op('pe', lambda e, c=c: e.matmul(
                    ps[:, bank, c * 128:(c + 1) * 128], lhsT=kT[g * 64:(g + 1) * 64, kcol:kcol + 128],
                    rhs=qT[g * 64:(g + 1) * 64, c, t * 128:(t + 1) * 128], start=True, stop=True),
                    reads=[kTB[kcol // 128], qTB[t]], writes=[PB[bank]], n=128)
            k.op('act', lambda e: e.activation(out=Pbuf[:, g, :], in_=psb(bank), func=AF.Exp),
                 reads=[PB[bank]], writes=[PB_[g]], n=512)
            k.op('pool', lambda e: e.tensor_tensor(
                out=Pbuf[:, g, :].rearrange("p (c n) -> p c n", n=128), in0=Pbuf[:, g, :].rearrange("p (c n) -> p c n", n=128),
                in1=bc(Cb(mask).unsqueeze(1), [128, 4, 128]), op=ALU.mult), reads=[PB_[g], cbB], writes=[PB_[g]], n=512)

        def attn_group_norm(t, g):
            k.op('dve', lambda e: e.tensor_tensor(
                out=rd[:, g, :].rearrange("p (c n) -> p c n", n=128), in0=bank4(A0),
                in1=bc(esink[:, g * 4:(g + 1) * 4].unsqueeze(2), [128, 4, 128]), op=ALU.add),
                reads=[PB[A0], derivB], writes=[rdB[g]], n=512)
            k.op('act', lambda e: e.activation(out=rd[:, g, :], in_=rd[:, g, :], func=AF.Ln), reads=[rdB[g]], writes=[rdB[g]], n=512)
            k.op('act', lambda e: e.activation(out=rd[:, g, :], in_=rd[:, g, :], func=AF.Exp, scale=-1.0),
                 reads=[rdB[g]], writes=[rdB[g]], n=512)
            k.op('dve', lambda e: e.tensor_tensor(
                out=oaT[g * 64:(g + 1) * 64, :, t * 128:(t + 1) * 128], in0=bank4(A1)[g * 64:(g + 1) * 64],
                in1=rd[g * 64:(g + 1) * 64, g, :].rearrange("p (c n) -> p c n", n=128), op=ALU.mult),
                reads=[PB[A1], rdB[g]], writes=[oaB[t]], n=512)

        def attn_prompt(t):
            has_prev = (t0 + t) > 0
            for g in range(2):
                attn_group_scores(t, g, A0, 128 + t * 128, Pc, PcB, 'M_cur')
                if has_prev:
                    attn_group_scores(t, g, A1, t * 128, Pp, PpB, 'M_prev')
                k.op('pe', lambda e, g=g: e.matmul(psb(A0), lhsT=Cb('ones'), rhs=Pc[:, g, :], start=True,
                                                   stop=not has_prev), reads=[PcB[g], cbB], writes=[PB[A0]], n=512)
                if has_prev:
                    k.op('pe', lambda e, g=g: e.matmul(psb(A0), lhsT=Cb('ones'), rhs=Pp[:, g, :], start=False, stop=True),
                         reads=[PpB[g], cbB], writes=[PB[A0]], n=512)
                for c in range(4):
                    k.op('pe', lambda e, g=g, c=c: e.matmul(
                        ps[:, A1, c * 128:(c + 1) * 128], lhsT=KV[:, 1 + t, 128:256], rhs=Pc[:, g, c * 128:(c + 1) * 128],
                        start=True, stop=not has_prev), reads=[KVB[1 + t], PcB[g]], writes=[PB[A1]], n=128)
                    if has_prev:
                        k.op('pe', lambda e, g=g, c=c: e.matmul(
                            ps[:, A1, c * 128:(c + 1) * 128], lhsT=KV[:, t, 128:256], rhs=Pp[:, g, c * 128:(c + 1) * 128],
                            start=False, stop=True), reads=[KVB[t], PpB[g]], writes=[PB[A1]], n=128)
                attn_group_norm(t, g)

        def attn_sample(t):
            for half in range(2):
                for b8 in range(8):
                    b = half * 8 + b8
                    k.op('pe', lambda e, b=b, b8=b8, half=half: e.transpose(
                        out=psb16(2 + half)[:, b8 * 128:(b8 + 1) * 128], in_=ck[:, b, :], identity=Cb('ident')),
                        reads=[ckB2[half], cbB], writes=[PB[2 + half]], n=128)
                k.op('act', lambda e, half=half: e.copy(
                    out=kcT[:, half * 8:(half + 1) * 8, :], in_=psb16(2 + half).rearrange("p (b n) -> p b n", n=128)),
                    reads=[PB[2 + half]], writes=[kcTB], n=1024)
            for g in range(2):
                attn_group_scores(t, g, A0, 128 + t * 128, Pc, PcB, 'M_blk')
                for b in range(16):
                    for c in range(4):
                        k.op('pe', lambda e, g=g, b=b, c=c: e.matmul(
                            ps[:, A1, c * 128 + 8 * b:c * 128 + 8 * b + 8],
                            lhsT=kcT[g * 64:(g + 1) * 64, b, :],
                            rhs=qT[g * 64:(g + 1) * 64, c, t * 128 + 8 * b:t * 128 + 8 * b + 8], start=True, stop=True),
                            reads=[kcTB, qTB[t]], writes=[PB[A1]], n=64)
                k.op('act', lambda e, g=g: e.activation(out=Pp[:, g, :], in_=psb(A1), func=AF.Exp),
                     reads=[PB[A1]], writes=[PpB[g]], n=512)
                k.op('pool', lambda e, g=g: e.tensor_tensor(
                    out=Pp[:, g, :].rearrange("p (x t) -> p x t", t=8), in0=Pp[:, g, :].rearrange("p (x t) -> p x t", t=8),
                    in1=bc(Cb('M_cache')[:, 0:8].unsqueeze(1), [128, 64, 8]), op=ALU.mult),
                    reads=[PpB[g], cbB], writes=[PpB[g]], n=512)
                k.op('pe', lambda e, g=g: e.matmul(psb(A0), lhsT=Cb('ones'), rhs=Pc[:, g, :], start=True, stop=False),
                     reads=[PcB[g], cbB], writes=[PB[A0]], n=512)
                k.op('pe', lambda e, g=g: e.matmul(psb(A0), lhsT=Cb('ones'), rhs=Pp[:, g, :], start=False, stop=True),
                     reads=[PpB[g], cbB], writes=[PB[A0]], n=512)
                for c in range(4):
                    k.op('pe', lambda e, g=g, c=c: e.matmul(
                        ps[:, A1, c * 128:(c + 1) * 128], lhsT=KV[:, 1 + t, 128:256], rhs=Pc[:, g, c * 128:(c + 1) * 128],
                        start=True, stop=False), reads=[KVB[1 + t], PcB[g]], writes=[PB[A1]], n=128)
                    for b in range(16):
                        k.op('pe', lambda e, g=g, c=c, b=b: e.matmul(
                            ps[:, A1, c * 128 + 8 * b:c * 128 + 8 * b + 8], lhsT=cv[:, b, :],
                            rhs=Pp[:, g, c * 128 + 8 * b:c * 128 + 8 * b + 8], start=False, stop=(b == 15)),
                            reads=[cvB2[b // 8], PpB[g]], writes=[PB[A1]], n=64)
                attn_group_norm(t, g)

        def dn_prep(t, sample, p):
            EG, EGB, QKd, QKdB, qd, qdB, kdn, kdnB, kdec, kdecB, Yw, YwB = DB[p]
            sfx = '_s' if sample else '_p'
            gt_ = gcol[:, t, :]
            bt_ = betac[:, t, :]
            k.op('dve', lambda e: e.tensor_tensor(out=Ug, in0=bc(C('U' + sfx).unsqueeze(1), [128, 4, 128]),
                                                  in1=bc(gt_.unsqueeze(2), [128, 4, 128]), op=ALU.mult),
                 reads=[cfB, gB], writes=[UgB])
            Ugf = Ug.rearrange("p h n -> p (h n)")
            k.op('pe', lambda e: e.matmul(psb(2), lhsT=C('SL' + sfx), rhs=Ugf, start=True, stop=True),
                 reads=[UgB, cfB], writes=[PB[2]])
            k.op('pe', lambda e: e.matmul(psb(3), lhsT=C('ones'), rhs=Ugf, start=True, stop=True),
                 reads=[UgB, cfB], writes=[PB[3]])
            k.op('dve', lambda e: e.tensor_tensor(out=E2m, in0=bank4(2), in1=bc(C('MB' + sfx).unsqueeze(1), [128, 4, 128]),
                                                  op=ALU.add), reads=[PB[2], cfB], writes=[E2mB])
            k.op('pe', lambda e: e.matmul(psb(2, 4), lhsT=C('SL' + sfx), rhs=gt_, start=True, stop=True),
                 reads=[gB, cfB], writes=[PB[2]])
            k.op('act', lambda e: e.activation(out=D2s, in_=E2m, func=AF.Exp), reads=[E2mB], writes=[D2sB])
            k.op('act', lambda e: e.activation(out=EG, in_=bank4(3), func=AF.Exp), reads=[PB[3]], writes=[EGB])
            k.op('act', lambda e: e.activation(out=kdsc, in_=psb(2, 4), func=AF.Exp), reads=[PB[2]], writes=[kdscB])
            k.op('pool', lambda e: e.tensor_tensor(out=D2b, in0=D2s, in1=bc(bt_.unsqueeze(2), [128, 4, 128]), op=ALU.mult),
                 reads=[D2sB, gB], writes=[D2bB])
            k.op('pool', lambda e: e.tensor_tensor(out=D2i, in0=D2s, in1=bc(C('ident').unsqueeze(1), [128, 4, 128]),
                                                   op=ALU.add), reads=[D2sB, cfB], writes=[D2iB])
            for h in range(4):
                k.op('pe', lambda e, h=h: e.matmul(
                    ps[:, 4 + h // 2, (h % 2) * 256:(h % 2) * 256 + 256], lhsT=KQV[:, t, h, 0, :],
                    rhs=KQV[:, t, h, 0:2, :], start=True, stop=True), reads=[KQVB[t]], writes=[PB[4 + h // 2]])
            GQ = ps[:, 4:6, :].rearrange("p b (x w n) -> p (b x) w n", x=2, w=2, n=128)
            k.op('dve', lambda e: e.tensor_tensor(out=Nn, in0=GQ[:, :, 0, :], in1=D2b, op=ALU.mult),
                 reads=[PB[4], PB[5], D2bB], writes=[NnB])
            k.op('dve', lambda e: e.tensor_tensor(out=QKd, in0=GQ[:, :, 1, :], in1=D2i, op=ALU.mult),
                 reads=[PB[4], PB[5], D2iB], writes=[QKdB])
            b16 = psb16(2).rearrange("p (c n) -> p c n", n=128)
            if DT_INV == F32:
                for h in range(4):
                    k.op('pe', lambda e, h=h: e.transpose(out=ps[:, 3, h * 128:(h + 1) * 128], in_=Nn[:, h, :], identity=C('ident')),
                         reads=[NnB, cfB], writes=[PB[3]])
                k.op('act', lambda e: e.copy(out=Nt, in_=bank4(3)), reads=[PB[3]], writes=[NtB])
            else:
                for h in range(4):
                    k.op('pe', lambda e, h=h: e.transpose(out=b16[:, h, :], in_=Nn[:, h, :], identity=Cb('ident')),
                         reads=[NnB, cbB], writes=[PB[2]])
                k.op('act', lambda e: e.copy(out=Nt, in_=b16[:, 0:4, :]), reads=[PB[2]], writes=[NtB])
            for h in range(4):
                k.op('pe', lambda e, h=h: e.transpose(out=b16[:, 4 + h, :], in_=KQV[:, t, h, 0, :], identity=Cb('ident')),
                     reads=[KQVB[t], cbB], writes=[PB[2]])
            k.op('dve', lambda e: e.tensor_tensor(out=kdec, in0=b16[:, 4:8, :], in1=bc(kdsc.unsqueeze(2), [128, 4, 128]),
                                                  op=ALU.mult), reads=[PB[2], kdscB], writes=[kdecB])
            k.op('pool', lambda e: e.tensor_tensor(out=Yw[0], in0=bc(Cb('ident').unsqueeze(1), [128, 4, 128]), in1=Nn,
                                                   op=ALU.subtract), reads=[NnB, cbB], writes=[YwB[0]])
            Pcur, PcurB, Ptcur, PtcurB = Nn, NnB, Nt, NtB
            yi = 0
            NLEV = 3 if sample else 6
            for lvl in range(1, NLEV + 1):
                w = lvl % 2
                for h in range(4):
                    k.op('pe', lambda e, h=h, Pcur=Pcur, Ptcur=Ptcur: e.matmul(
                        ps[:, 3, h * 128:(h + 1) * 128], lhsT=Pcur[:, h, :], rhs=Ptcur[:, h, :], start=True, stop=True),
                        reads=[PcurB, PtcurB], writes=[PB[3]], n=128, c=(0.27 if (lvl <= 2 or not MIXP) else 0.11))
                if lvl < NLEV:
                    for h in range(4):
                        k.op('pe', lambda e, h=h, Pcur=Pcur, Ptcur=Ptcur: e.matmul(
                            ps[:, 4, h * 128:(h + 1) * 128], lhsT=Ptcur[:, h, :], rhs=Pcur[:, h, :], start=True, stop=True),
                            reads=[PcurB, PtcurB], writes=[PB[4]], n=128, c=(0.27 if (lvl <= 2 or not MIXP) else 0.11))
                if lvl < 2 or MIXP == 0:
                    dPt, dPtB, dP, dPB = Ptw[w], PtwB[w], Pw[w], PwB[w]
                else:
                    dPt, dPtB, dP, dPB = Ptb[w], PtbB[w], Pb[w], PbB[w]
                k.op('act', lambda e, dPt=dPt: e.copy(out=dPt, in_=bank4(3)), reads=[PB[3]], writes=[dPtB], n=512)
                if lvl < NLEV:
                    k.op('dve', lambda e, dP=dP: e.tensor_copy(out=dP, in_=bank4(4)), reads=[PB[4]], writes=[dPB], n=512)
                if lvl < 2 or MIXP == 0:
                    yr, yrB = Yw[yi], YwB[yi]
                else:
                    yr, yrB = Yb, YbB
                for h in range(4):
                    k.op('pe', lambda e, h=h, dPt=dPt, yr=yr: e.matmul(
                        ps[:, 5, h * 128:(h + 1) * 128], lhsT=dPt[:, h, :], rhs=yr[:, h, :], start=True, stop=True),
                        reads=[dPtB, yrB], writes=[PB[5]], n=128, c=(0.27 if (lvl < 2 or not MIXP) else 0.11))
                k.op('dve', lambda e, yi=yi: e.tensor_tensor(out=Yw[1 - yi], in0=bank4(5), in1=Yw[yi], op=ALU.add),
                     reads=[PB[5], YwB[yi]], writes=[YwB[1 - yi]], n=512)
                yi = 1 - yi
                if lvl < NLEV and MIXP:
                    k.op('act', lambda e, yi=yi: e.copy(out=Yb, in_=Yw[yi]), reads=[YwB[yi]], writes=[YbB], n=512)
                Pcur, PcurB, Ptcur, PtcurB = dP, dPB, dPt, dPtB
            k.op('dve', lambda e: e.tensor_tensor(out=qd, in0=KQV[:, t, :, 1, :], in1=EG, op=ALU.mult),
                 reads=[KQVB[t], EGB], writes=[qdB])
            k.op('dve', lambda e: e.scalar_tensor_tensor(out=kdn, in0=KQV[:, t, :, 0, :], scalar=-1.0, in1=EG,
                                                         op0=ALU.mult, op1=ALU.mult), reads=[KQVB[t], EGB], writes=[kdnB])
            return yi

        def dn_out(t):
            k.op('act', lambda e: e.activation(out=osq, in_=bank4(6), func=AF.Square), reads=[PB[6]], writes=[osqB])
            k.op('pe', lambda e: e.matmul(psb(7), lhsT=Cb('ones'), rhs=osq.rearrange("p h n -> p (h n)"), start=True, stop=True),
                 reads=[osqB, cbB], writes=[PB[7]])
            k.op('act', lambda e: e.activation(out=orn, in_=bank4(7), func=AF.Ln, bias=EPS, scale=1.0 / 128),
                 reads=[PB[7]], writes=[ornB])
            k.op('act', lambda e: e.activation(out=orn, in_=orn, func=AF.Exp, scale=-0.5), reads=[ornB], writes=[ornB])
            k.op('dve', lambda e: e.tensor_tensor(out=otmp, in0=bank4(6), in1=orn, op=ALU.mult),
                 reads=[PB[6], ornB], writes=[otmpB])
            k.op('dve', lambda e: e.scalar_tensor_tensor(
                out=obT[:, :, t * 128:(t + 1) * 128], in0=otmp, scalar=smallp[:, 16:17],
                in1=zsT[:, :, t * 128:(t + 1) * 128], op0=ALU.mult, op1=ALU.mult),
                reads=[otmpB, zsB[t], smallpB], writes=[obB[t]])

        def dn_scan_prompt(t, wi, p):
            EG, EGB, QKd, QKdB, qd, qdB, kdn, kdnB, kdec, kdecB, Yw, YwB = DB[p]
            WT, WTB = Yw[wi], YwB[wi]
            bt_ = betac[:, t, :]
            for h in range(4):
                k.op('pe', lambda e, h=h: e.matmul(ps[:, 6, h * 128:(h + 1) * 128], lhsT=KQV[:, t, h, 2, :], rhs=Cb('ident'),
                                                   start=True, stop=False), reads=[KQVB[t], cbB], writes=[PB[6]])
                k.op('pe', lambda e, h=h: e.matmul(ps[:, 6, h * 128:(h + 1) * 128], lhsT=kdn[:, h, :], rhs=Sbf[:, h, :],
                                                   start=False, stop=True), reads=[kdnB, SbfB], writes=[PB[6]])
            k.op('act', lambda e: e.copy(out=Xs, in_=bank4(6)), reads=[PB[6]], writes=[XsB])
            for h in range(4):
                k.op('pe', lambda e, h=h: e.matmul(ps[:, 7, h * 128:(h + 1) * 128], lhsT=WT[:, h, :], rhs=Xs[:, h, :],
                                                   start=True, stop=True), reads=[WTB, XsB], writes=[PB[7]])
            k.op('dve', lambda e: e.tensor_tensor(out=vnew, in0=bank4(7), in1=bc(bt_.unsqueeze(2), [128, 4, 128]),
                                                  op=ALU.mult), reads=[PB[7], gB], writes=[vnewB])
            for h in range(4):
                k.op('pe', lambda e, h=h: e.matmul(ps[:, 6, h * 128:(h + 1) * 128], lhsT=Sbf[:, h, :], rhs=qd[:, h, :],
                                                   start=True, stop=False), reads=[SbfB, qdB], writes=[PB[6]])
                k.op('pe', lambda e, h=h: e.matmul(ps[:, 6, h * 128:(h + 1) * 128], lhsT=vnew[:, h, :], rhs=QKd[:, h, :],
                                                   start=False, stop=True), reads=[vnewB, QKdB], writes=[PB[6]])
            for h in range(4):
                k.op('pe', lambda e, h=h: e.matmul(ps[:, 7, h * 128:(h + 1) * 128], lhsT=kdec[:, h, :], rhs=vnew[:, h, :],
                                                   start=True, stop=True), reads=[kdecB, vnewB], writes=[PB[7]])
            k.op('dve', lambda e: e.tensor_tensor(out=tmpS, in0=S, in1=bc(EG[:, :, 127:128], [128, 4, 128]), op=ALU.mult),
                 reads=[SB, EGB], writes=[tmpSB])
            k.op('dve', lambda e: e.tensor_tensor(out=S, in0=bank4(7), in1=tmpS, op=ALU.add),
                 reads=[PB[7], tmpSB], writes=[SB])
            k.op('act', lambda e: e.copy(out=Sbf, in_=S), reads=[SB], writes=[SbfB])
            dn_out(t)

        def dn_scan_sample(t, wi, p):
            EG, EGB, QKd, QKdB, qd, qdB, kdn, kdnB, kdec, kdecB, Yw, YwB = DB[p]
            WT, WTB = Yw[wi], YwB[wi]
            bt_ = betac[:, t, :]
            SstH = [Buf(), Buf()]
            SsbH = [Buf(), Buf()]
            for h in range(4):
                for hf in range(2):
                    bs = slice(hf * 8, (hf + 1) * 8)
                    extra = [ckB2[hf], cvB2[hf]] if h == 0 else []
                    k.dma('sp', Sst[:, bs, :], st_delta[bs, h].rearrange("b d e -> d b e"), writes=[SstH[hf]] + extra,
                          nbytes=1 << 19)
                    k.op('act', lambda e, bs=bs: e.copy(out=Ssb[:, bs, :], in_=Sst[:, bs, :]), reads=[SstH[hf]],
                         writes=[SsbH[hf]] + ([kcTB] if h == 0 else []), n=1024)
                Zf = Zk.rearrange("p b i -> p (b i)")
                k.op('dve', lambda e, h=h, Zf=Zf: e.tensor_copy(
                    out=Zf[:, 0:2040].rearrange("p (b r) -> p b r", r=136)[:, :, 0:8],
                    in_=kdn[:, h, 0:120].rearrange("p (b t) -> p b t", t=8)), reads=[kdnB], writes=[ZkB], n=128)
                k.op('dve', lambda e, h=h, Zf=Zf: e.tensor_copy(out=Zf[:, 2040:2048], in_=kdn[:, h, 120:128]),
                     reads=[kdnB], writes=[ZkB], n=64)
                k.op('pool', lambda e, h=h: e.tensor_tensor(
                    out=kdm, in0=bc(kdec[:, h, :].unsqueeze(1), [128, 16, 128]),
                    in1=bc(Cb('BM')[:, 0:16].unsqueeze(2), [128, 16, 128]), op=ALU.mult),
                    reads=[kdecB, cbB], writes=[kdmB], n=2048)
                k.op('pe', lambda e, h=h: e.matmul(ps[:, 6, h * 128:(h + 1) * 128], lhsT=KQV[:, t, h, 2, :], rhs=Cb('ident'),
                                                   start=True, stop=False), reads=[KQVB[t], cbB], writes=[PB[6]], n=128)
                for b in range(16):
                    k.op('pe', lambda e, h=h, b=b: e.matmul(ps[:, 6, h * 128:(h + 1) * 128], lhsT=Zk[:, b, :], rhs=Ssb[:, b, :],
                                                            start=False, stop=(b == 15)), reads=[ZkB, SsbH[b // 8]],
                         writes=[PB[6]], n=128)
                k.op('act', lambda e, h=h: e.copy(out=Xs[:, h, :], in_=ps[:, 6, h * 128:(h + 1) * 128]),
                     reads=[PB[6]], writes=[XsB], n=128)
                k.op('pe', lambda e, h=h: e.matmul(ps[:, 7, h * 128:(h + 1) * 128], lhsT=WT[:, h, :], rhs=Xs[:, h, :],
                                                   start=True, stop=True), reads=[WTB, XsB], writes=[PB[7]], n=128, c=0.27)
                k.op('dve', lambda e, h=h: e.tensor_scalar(out=vnew[:, h, :], in0=ps[:, 7, h * 128:(h + 1) * 128],
                                                           scalar1=bt_[:, h:h + 1], scalar2=None, op0=ALU.mult),
                     reads=[PB[7], gB], writes=[vnewB], n=128)
                for b in range(16):
                    k.op('pe', lambda e, h=h, b=b: e.matmul(
                        ps[:, 6, h * 128 + 8 * b:h * 128 + 8 * b + 8], lhsT=Ssb[:, b, :], rhs=qd[:, h, 8 * b:8 * b + 8],
                        start=(b == 0), stop=False), reads=[SsbH[b // 8], qdB], writes=[PB[6]], n=64)
                k.op('pe', lambda e, h=h: e.matmul(ps[:, 6, h * 128:(h + 1) * 128], lhsT=vnew[:, h, :], rhs=QKd[:, h, :],
                                                   start=False, stop=True), reads=[vnewB, QKdB], writes=[PB[6]], n=128)
                for b in range(16):
                    k.op('pe', lambda e, h=h, b=b: e.matmul(
                        ps[:, b // 4, (b % 4) * 128:(b % 4) * 128 + 128], lhsT=kdm[:, b, :], rhs=vnew[:, h, :],
                        start=True, stop=True), reads=[kdmB, vnewB], writes=[PB[b // 4]], n=128)
                for b in range(16):
                    k.op('dve', lambda e, h=h, b=b: e.scalar_tensor_tensor(
                        out=Sst[:, b, :], in0=Sst[:, b, :], scalar=EG[:, h, 8 * b + 7:8 * b + 8],
                        in1=ps[:, b // 4, (b % 4) * 128:(b % 4) * 128 + 128], op0=ALU.mult, op1=ALU.add),
                        reads=[SstH[b // 8], EGB, PB[b // 4]], writes=[SstH[b // 8]], n=128)
                for hf in range(2):
                    bs = slice(hf * 8, (hf + 1) * 8)
                    k.dma('sp', nd_s[bs, h].rearrange("b d e -> d b e"), Sst[:, bs, :], reads=[SstH[hf]], writes=[outB],
                          nbytes=1 << 19)
            dn_out(t)

        for t in range(nt):
            p = (t % 2) if NPAR == 2 else 0
            if t < npt:
                attn_prompt(t)
                wi = dn_prep(t, False, p)
                dn_scan_prompt(t, wi, p)
                if t0 + t == 15:
                    k.dma('sp', nd_p.rearrange("h d e -> d h e"), S, reads=[SB], writes=[outB])
            else:
                attn_sample(t)
                wi = dn_prep(t, True, p)
                dn_scan_sample(t, wi, p)
        if not has_s:
            k.op('pool', lambda e: e.tensor_copy(out=kT[:, 0:128], in_=kT[:, npt * 128:npt * 128 + 128]),
                 reads=[kTB[npt]], writes=[kTB[0]])
            k.op('pool', lambda e: e.tensor_copy(out=KV[:, 0, :], in_=KV[:, npt, :]), reads=[KVB[npt]], writes=[KVB[0]])
        for blk_ in range(pre3):
            m3_wload(blk_)
        k.barrier()
        A.release(m2)
        if stop_after == 'm2':
            A.release(m0)
            return

        m3 = A.mark()
        post = make_post()
        wo = A.alloc([8, D], BF16)
        woB = Buf()
        k.dma('pool', wo, w_o.rearrange("(c p) n -> p c n", p=128), writes=[woB])
        mT = A.alloc([8, NTMAX * 128], BF16)
        mTB = [Buf() for _ in range(8)]
        wga = [WS[0], WS[1]]
        wgb = [WS[2], WS[3]]
        wa = [WS[4][:, 0:4, :], WS[4][:, 4:8, :]]
        wb = [WS[5][:, 0:4, :], WS[5][:, 4:8, :]]
        sga = [A.alloc([512], F32) for _ in range(2)]
        sgb = [A.alloc([512], F32) for _ in range(2)]
        mm1 = [A.alloc([512], F32) for _ in range(2)]
        mm2 = [A.alloc([512], F32) for _ in range(2)]
        sgaB, sgbB, mm1B, mm2B = [[Buf(), Buf()] for _ in range(4)]
        gi = 0
        for blk in range(4):
            s = blk % 2
            if blk >= pre3:
                m3_wload(blk)
            for jj in range(2):
                c8 = blk * 2 + jj
                cs = slice(jj * 128, (jj + 1) * 128)
                for (o, n) in groups:
                    b0 = (gi % 2) * 4
                    x = gi % 2
                    gi += 1
                    for c in range(8):
                        k.op('pe', lambda e, c=c, s=s, cs=cs, o=o, n=n, b0=b0: e.matmul(
                            psb(b0, n), lhsT=wga[s][:, c, cs], rhs=hT[:, c, o:o + n], start=(c == 0), stop=(c == 7)),
                            reads=[WSB[s]] + tb(hTB, o, n), writes=[PB[b0]], n=n)
                    for c in range(8):
                        k.op('pe', lambda e, c=c, s=s, cs=cs, o=o, n=n, b0=b0: e.matmul(
                            psb(b0 + 1, n), lhsT=wgb[s][:, c, cs], rhs=hT[:, c, o:o + n], start=(c == 0), stop=(c == 7)),
                            reads=[WSB[2 + s]] + tb(hTB, o, n), writes=[PB[b0 + 1]], n=n)
                    for c in range(4):
                        k.op('pe', lambda e, c=c, s=s, cs=cs, o=o, n=n, b0=b0: e.matmul(
                            psb(b0 + 2, n), lhsT=wa[s][:, c, cs], rhs=oaT[:, c, o:o + n], start=(c == 0), stop=(c == 3)),
                            reads=[WSB[4]] + tb(oaB, o, n), writes=[PB[b0 + 2]], n=n)
                    for c in range(4):
                        k.op('pe', lambda e, c=c, s=s, cs=cs, o=o, n=n, b0=b0: e.matmul(
                            psb(b0 + 3, n), lhsT=wb[s][:, c, cs], rhs=obT[:, c, o:o + n], start=(c == 0), stop=(c == 3)),
                            reads=[WSB[5]] + tb(obB, o, n), writes=[PB[b0 + 3]], n=n)
                    k.op('act', lambda e, x=x, n=n, b0=b0: e.activation(out=sga[x][:, 0:n], in_=psb(b0, n), func=AF.Sigmoid),
                         reads=[PB[b0]], writes=[sgaB[x]])
                    k.op('act', lambda e, x=x, n=n, b0=b0: e.activation(out=sgb[x][:, 0:n], in_=psb(b0 + 1, n), func=AF.Sigmoid),
                         reads=[PB[b0 + 1]], writes=[sgbB[x]])
                    k.op('dve', lambda e, x=x, n=n, b0=b0: e.tensor_tensor(out=mm1[x][:, 0:n], in0=psb(b0 + 2, n),
                                                                           in1=sga[x][:, 0:n], op=ALU.mult),
                         reads=[PB[b0 + 2], sgaB[x]], writes=[mm1B[x]])
                    k.op('dve', lambda e, x=x, n=n, b0=b0: e.tensor_tensor(out=mm2[x][:, 0:n], in0=psb(b0 + 3, n),
                                                                           in1=sgb[x][:, 0:n], op=ALU.mult),
                         reads=[PB[b0 + 3], sgbB[x]], writes=[mm2B[x]])
                    k.op('pool', lambda e, x=x, n=n, o=o, c8=c8: e.tensor_tensor(out=mT[:, c8, o:o + n], in0=mm1[x][:, 0:n],
                                                                                 in1=mm2[x][:, 0:n], op=ALU.add),
                         reads=[mm1B[x], mm2B[x]], writes=[mTB[c8]])
        for t in range(nt):
            banks = [0, 1] if t % 2 == 0 else [2, 3]
            for hh in range(2):
                for c in range(8):
                    k.op('pe', lambda e, t=t, hh=hh, c=c, banks=banks: e.matmul(
                        psb(banks[hh]), lhsT=mT[:, c, t * 128:(t + 1) * 128], rhs=wo[:, c, hh * 512:(hh + 1) * 512],
                        start=(c == 0), stop=(c == 7)), reads=[mTB[c], woB], writes=[PB[banks[hh]]])
            post(t, banks, 1, False)
        for fb_ in range(next_ffn_pre):
            ffn_wload(1, fb_)
        A.release(m0)

    thirds = [(0, 6, False), (6, 6, False), (12, 4, True)]
    def load_x(t0, npt, has_s):
        for t in range(npt):
            k.dma('sp', X[:, t, :], x_p[(t0 + t) * 128:(t0 + t + 1) * 128, :], writes=[XB[t]], nbytes=1 << 19)
        if has_s:
            k.dma('sp', X[:, npt, :], x_s[:, :], writes=[XB[npt]], nbytes=1 << 19)

    load_x(*thirds[0])
    for ti, (t0, npt, has_s) in enumerate(thirds):
        nt = npt + (1 if has_s else 0)
        PF = 0 if 'P' in DBG else 1
        if not skip_ffn:
            ffn(nt, 0, pre=(2 * PF if t0 > 0 else 0))
        if stop_after != 'ffn1':
            for b_ in range(2 * PF):
                m1_wload(b_)
        k.barrier()
        if stop_after != 'ffn1':
            mixer(nt, npt, has_s, t0, pre1=2 * PF, pre3=PF, next_ffn_pre=(2 * PF if not skip_ffn else 0))
            k.barrier()
            if not skip_ffn:
                ffn(nt, 1, pre=2 * PF)
                if not has_s:
                    for fb_ in range(2 * PF):
                        ffn_wload(0, fb_)
            if skip_ffn:
                k.barrier()
        oB = Buf()
        for t in range(npt):
            k.dma('sp', y_p[(t0 + t) * 128:(t0 + t + 1) * 128, :], X[:, t, :], reads=[XB[t]], writes=[oB])
        if has_s:
            k.dma('sp', y_s[:, :], X[:, npt, :], reads=[XB[npt]], writes=[oB])
        if ti + 1 < len(thirds):
            load_x(*thirds[ti + 1])
        if skip_ffn or stop_after is not None or ti + 1 == len(thirds):
            k.barrier()
    k.barrier()
    k.emit()
    return nc


def prep_inputs(inp):
    f = lambda a: np.ascontiguousarray(np.asarray(a, dtype=np.float32))
    names, carr, cm = make_consts()
    w_in = f(inp['w_in'][0])
    offs = np.cumsum([0, 512, 128, 128, 1536, 512, 4, 4, 1024, 1024])
    qa, ka, va, qkvb, zb, ab, bb, ga, gb = [w_in[:, offs[i]:offs[i + 1]] for i in range(9)]
    perm = np.concatenate([np.r_[c * 64:(c + 1) * 64, (4 + c) * 64:(5 + c) * 64] for c in range(4)])
    w_fm = f(np.concatenate([qa[:, perm], ka, qkvb, zb, ga, gb], axis=1))
    w_tm = f(np.concatenate([ka, va, ab, bb, np.zeros((D, 56), np.float32)], axis=1))
    w_ba = f(inp['w_branch_a'][0][perm, :])
    gpost = f(np.broadcast_to(np.stack([inp['ffn1_norm_post'][0], inp['mix_norm_post'][0], inp['ffn2_norm_post'][0]])[None],
                              (128, 3, D)))
    gpre = np.stack([inp['ffn1_norm_pre'][0], inp['mix_norm_pre'][0], inp['ffn2_norm_pre'][0]])
    gpreT = f(gpre.reshape(3, 8, 128).transpose(2, 0, 1))
    smallp = np.zeros((128, 32), np.float32)
    smallp[:, 0:4] = np.asarray(inp['dn_a_log'][0])[None]
    smallp[:, 4:8] = np.asarray(inp['dn_dt_bias'][0])[None]
    smallp[:, 8:16] = np.asarray(inp['attn_sinks'][0])[None]
    smallp[:, 16] = np.asarray(inp['dn_out_norm'][0])
    convw = f(np.asarray(inp['conv_w'][0]).reshape(4, 12, 128).transpose(2, 1, 0))
    shared = {
        'consts': carr, 'colmask': cm, 'gpost': gpost, 'gpreT': gpreT, 'smallp': smallp, 'convw': convw,
        'w_up1': f(inp['ffn1_w_up'][0]), 'w_up2': f(inp['ffn2_w_up'][0]),
        'w_down1': f(inp['ffn1_w_down'][0]), 'w_down2': f(inp['ffn2_w_down'][0]),
        'w_fm': w_fm, 'w_tm': w_tm, 'w_ba': w_ba, 'w_bb': f(inp['w_branch_b'][0]), 'w_o': f(inp['w_out'][0]),
    }
    maps = []
    for c in range(NCORE):
        m = dict(shared)
        m['x_p'] = f(inp['x_prompt'][c])
        m['x_s'] = f(np.asarray(inp['x_sample'])[16 * c:16 * c + 16].reshape(128, D))
        m['cache_k'] = f(np.asarray(inp['cache_swa_k'])[0, 16 * c:16 * c + 16].reshape(16, 128, 128))
        m['cache_v'] = f(np.asarray(inp['cache_swa_v'])[0, 16 * c:16 * c + 16].reshape(16, 128, 128))
        m['st_conv'] = f(np.asarray(inp['state_conv'])[0, 16 * c:16 * c + 16].reshape(48, 1536))
        m['st_delta'] = f(np.asarray(inp['state_delta'])[0, 16 * c:16 * c + 16])
        maps.append(m)
    return maps


_NC_CACHE = {}


def kernel(**inputs):
    maps = prep_inputs(inputs)
    if 'nc' not in _NC_CACHE:
        _NC_CACHE['nc'] = build_nc()
    nc = _NC_CACHE['nc']
    res = run_bass_kernel_spmd(nc, maps, core_ids=list(range(NCORE)))
    R = res.results
    g = lambda n: np.stack([np.asarray(r[n], dtype=np.float32) for r in R])
    y_p = g('y_p')
    y_s = g('y_s').reshape(128, 8, D)
    nk_p = g('nk_p').reshape(1, 8, 128, 2, 64)
    nv_p = g('nv_p').reshape(1, 8, 128, 2, 64)
    ncv_p = g('ncv_p').reshape(1, 8, 3, 1536)
    nd_p = g('nd_p').reshape(1, 8, 4, 128, 128)
    nk_s = g('nk_s').reshape(1, 128, 128, 2, 64)
    nv_s = g('nv_s').reshape(1, 128, 128, 2, 64)
    ncv_s = g('ncv_s').reshape(1, 128, 3, 1536)
    nd_s = g('nd_s').reshape(1, 128, 4, 128, 128)
    return (y_p, y_s, nk_p, nv_p, ncv_p, nd_p, nk_s, nv_s, ncv_s, nd_s)
```
